# Optimizing a Trainium2 kernel written in Bass

```python
import jax, jax.numpy as jnp
from jax import lax
import numpy as np

D_MODEL = 1024
BATCH = 8
SEQ = 4096
DEPTH = 1

CHUNK = 64
N_META = 16
HG_DK = 128
HG_DV = 128
HG_HEADS = D_MODEL // HG_DK
HG_WIDTH = HG_HEADS * HG_DK
SUB = 16
POOL_WINDOWS = (2, 4, 8, 16)
POOL_GROUPS = 4
POOL_WIDTH = D_MODEL // 2
POOL_GC = POOL_WIDTH // POOL_GROUPS
D_FF = -(-8 * D_MODEL // (3 * 256)) * 256
EPS = 1e-6
IN_COLS = 4 * HG_WIDTH + POOL_WIDTH + 2 * D_MODEL
SPLIT_IDX = (HG_WIDTH, 2 * HG_WIDTH, 3 * HG_WIDTH, 4 * HG_WIDTH,
             4 * HG_WIDTH + POOL_WIDTH, 4 * HG_WIDTH + POOL_WIDTH + D_MODEL)

kernel_name = "hgrn2_pool_gated_hybrid_block"


def rmsnorm(x, g):
    xf = x.astype(jnp.float32)
    y = xf * lax.rsqrt(jnp.mean(xf * xf, axis=-1, keepdims=True) + EPS)
    return (y * g.astype(jnp.float32)).astype(x.dtype)


def _hgrn2_chunk(state, inp):
    q, k, v, lg = inp
    bsz, nh, c, dk = q.shape
    n = c // SUB
    b = jnp.cumsum(lg, axis=2)
    b_last = b[:, :, -1]
    o_inter = jnp.einsum('bhck,bhkv->bhcv', q * jnp.exp(b), state)
    qr = q.reshape(bsz, nh, n, SUB, dk)
    kr = k.reshape(bsz, nh, n, SUB, dk)
    vr = v.reshape(bsz, nh, n, SUB, -1)
    br = b.reshape(bsz, nh, n, SUB, dk)
    tri = jnp.tril(jnp.ones((SUB, SUB), dtype=bool))
    diff = br[:, :, :, :, None, :] - br[:, :, :, None, :, :]
    decay = jnp.exp(jnp.where(tri[:, :, None], diff, -jnp.inf))
    a_diag = jnp.einsum('bhntd,bhnsd,bhntsd->bhnts', qr, kr, decay)
    e = br[:, :, :, -1]
    q_off = qr[:, :, :, :, None, :] * jnp.exp(
        jnp.minimum(br[:, :, :, :, None, :] - e[:, :, None, None, :, :], 0.0))
    k_off = kr * jnp.exp(e[:, :, :, None, :] - br)
    a_off = jnp.einsum('bhitjd,bhjsd->bhitjs', q_off, k_off)
    lower = jnp.tril(jnp.ones((n, n), jnp.float32), -1)[None, None, :, None, :, None]
    eye = jnp.eye(n, dtype=jnp.float32)[None, None, :, None, :, None]
    a = a_off * lower + a_diag[:, :, :, :, None, :] * eye
    o_intra = jnp.einsum('bhitjs,bhjsv->bhitv', a, vr).reshape(bsz, nh, c, -1)
    new_state = jnp.exp(b_last)[..., None] * state + jnp.einsum(
        'bhck,bhcv->bhkv', k * jnp.exp(b_last[:, :, None] - b), v)
    return new_state, o_inter + o_intra


def hgrn2_mixer(q_pre, f_pre, i_pre, lb):
    f32 = jnp.float32
    bsz, t_len, _ = q_pre.shape
    lbf = lb.astype(f32)
    fp = f_pre.astype(f32)
    q = jax.nn.silu(q_pre.astype(f32))
    lg = jnp.logaddexp(jnp.log(lbf), jnp.log1p(-lbf) + jax.nn.log_sigmoid(fp))
    k = (1.0 - lbf) * jax.nn.sigmoid(-fp)
    v = i_pre.astype(f32)
    pad = (-t_len) % CHUNK
    padt = lambda a: jnp.pad(a, ((0, 0), (pad, 0), (0, 0)))
    q, k, v, lg = padt(q), padt(k), padt(v), padt(lg)
    n_chunks = (t_len + pad) // CHUNK
    to_chunks = lambda a, dh: a.reshape(bsz, n_chunks, CHUNK, HG_HEADS, dh).transpose(1, 0, 3, 2, 4)
    xs = (to_chunks(q, HG_DK), to_chunks(k, HG_DK), to_chunks(v, HG_DV), to_chunks(lg, HG_DK))
    state0 = jnp.zeros((bsz, HG_HEADS, HG_DK, HG_DV), f32)
    _, o = lax.scan(_hgrn2_chunk, state0, xs)
    o = o.transpose(1, 0, 3, 2, 4).reshape(bsz, n_chunks * CHUNK, HG_HEADS * HG_DV)
    return o[:, pad:]


def pool_mixer(xp, w_grp, scale):
    f32 = jnp.float32
    bsz, t_len, _ = xp.shape
    xf = xp.astype(f32)
    cs = jnp.pad(jnp.cumsum(xf, axis=1), ((0, 0), (1, 0), (0, 0)))
    pos = jnp.arange(t_len)
    outs = []
    for g, w in enumerate(POOL_WINDOWS):
        sl = slice(g * POOL_GC, (g + 1) * POOL_GC)
        c = cs[:, :, sl]
        lagged = jnp.pad(c, ((0, 0), (w - 1, 0), (0, 0)))[:, :t_len]
        cnt = jnp.minimum(pos + 1, w).astype(f32)[None, :, None]
        outs.append((c[:, 1:] - lagged) / cnt - xf[:, :, sl])
    pooled = jnp.concatenate(outs, axis=-1).reshape(bsz, t_len, POOL_GROUPS, POOL_GC)
    y = jnp.einsum('btgc,gcd->btgd', pooled, w_grp.astype(f32)).reshape(bsz, t_len, POOL_WIDTH)
    return (y * scale.astype(f32)).astype(xp.dtype)


def setup_inputs(seed: int = 0) -> dict:
    key = jax.random.key(seed)
    ks = jax.random.split(key, 16)
    f32 = jnp.float32

    def w(k, shape, fan_in):
        return jax.random.normal(k, shape, f32) * (fan_in ** -0.5)

    def gain(k, shape):
        return 1.0 + 0.05 * jax.random.normal(k, shape, f32)

    return {
        "x": jax.random.normal(ks[0], (BATCH, SEQ, D_MODEL), f32),
        "meta_tokens": jax.random.normal(ks[1], (N_META, D_MODEL), f32),
        "lb_logits": 0.1 * jax.random.normal(ks[2], (DEPTH + 1, HG_WIDTH), f32),
        "norm_mix_g": gain(ks[3], (DEPTH, D_MODEL)),
        "w_in": w(ks[4], (DEPTH, D_MODEL, IN_COLS), D_MODEL),
        "hg_norm_g": gain(ks[5], (DEPTH, HG_WIDTH)),
        "w_pool_grp": w(ks[6], (DEPTH, POOL_GROUPS, POOL_GC, POOL_GC), POOL_GC),
        "pool_scale": 1.0 + 0.1 * jax.random.normal(ks[7], (DEPTH, POOL_WIDTH), f32),
        "w_br_hgrn": w(ks[8], (DEPTH, HG_WIDTH, D_MODEL), HG_WIDTH),
        "w_br_pool": w(ks[9], (DEPTH, POOL_WIDTH, D_MODEL), POOL_WIDTH),
        "w_out": w(ks[10], (DEPTH, D_MODEL, D_MODEL), D_MODEL),
        "norm_ffn_g": gain(ks[11], (DEPTH, D_MODEL)),
        "w_ffn_gate": w(ks[12], (DEPTH, D_MODEL, D_FF), D_MODEL),
        "w_ffn_up": w(ks[13], (DEPTH, D_MODEL, D_FF), D_MODEL),
        "w_ffn_down": w(ks[14], (DEPTH, D_FF, D_MODEL), D_FF),
        "final_norm_g": gain(ks[15], (D_MODEL,)),
    }


def reference(x, meta_tokens, lb_logits, norm_mix_g, w_in, hg_norm_g, w_pool_grp, pool_scale,
              w_br_hgrn, w_br_pool, w_out, norm_ffn_g, w_ffn_gate, w_ffn_up, w_ffn_down,
              final_norm_g):
    f32 = jnp.float32
    bsz = x.shape[0]
    meta = jnp.broadcast_to(meta_tokens.astype(x.dtype)[None], (bsz, N_META, D_MODEL))
    h = jnp.concatenate([meta, x], axis=1)
    t_len = h.shape[1]
    lbs = jnp.cumsum(jax.nn.softmax(lb_logits.astype(f32), axis=0), axis=0)
    for l in range(DEPTH):
        u = rmsnorm(h, norm_mix_g[l])
        proj = u @ w_in[l]
        q_pre, f_pre, i_pre, og_pre, xp, ga, gb = jnp.split(proj, SPLIT_IDX, axis=-1)
        o = hgrn2_mixer(q_pre, f_pre, i_pre, lbs[l])
        o = rmsnorm(o.reshape(bsz, t_len, HG_HEADS, HG_DV),
                    hg_norm_g[l].reshape(HG_HEADS, HG_DV)).reshape(bsz, t_len, HG_WIDTH)
        ya = (o * jax.nn.sigmoid(og_pre.astype(f32))).astype(h.dtype) @ w_br_hgrn[l]
        yb = pool_mixer(xp, w_pool_grp[l], pool_scale[l]) @ w_br_pool[l]
        mixed = jax.nn.sigmoid(ga) * ya + jax.nn.sigmoid(gb) * yb
        h = h + mixed @ w_out[l]
        u = rmsnorm(h, norm_ffn_g[l])
        h = h + (jax.nn.silu(u @ w_ffn_gate[l]) * (u @ w_ffn_up[l])) @ w_ffn_down[l]
    return rmsnorm(h, final_norm_g)[:, N_META:]
```

```python
import numpy as np
import concourse.bass as bass
import concourse.mybir as mybir
from concourse.bass_utils import run_bass_kernel_spmd

F32 = mybir.dt.float32
BF16 = mybir.dt.bfloat16
AF = mybir.ActivationFunctionType
ALU = mybir.AluOpType

D = 1024
SEQ = 4096
NMETA = 16
TT = SEQ + NMETA
NTILES = 9
NTM = 460
SIZES = [457] * 8 + [456]
OFFS = [sum(SIZES[:i]) for i in range(NTILES)]
DFF = 2816
NFC = DFF // 128
EPS = 1e-6
NSLOT = 6
SLOTW = 4096
ENG = ("pe", "act", "dve", "pool", "sp")

C_MASK, C_WC = 0, 128
C_TOT = C_WC + 64
CB_IDENT, CB_ONES = 0, 128
CB_TOT = 256
P_G1, P_G2, P_G3, P_GHG, P_PSC, P_L0, P_L1 = 0, 8, 16, 24, 32, 36, 44
P_TOT = 52


def chunks_of(nt):
    base = nt // 4
    rem = nt % 4
    out = []
    c0 = 0
    for i in range(4):
        c = base + (1 if i < rem else 0)
        out.append((c0, c))
        c0 += c
    return out


class Prog:
    def __init__(self):
        self.q = {e: [] for e in ENG}
        self.cnt = {}
        self.waited = {e: {} for e in ENG}
        self.recs = {}

    def emit(self, eng, fn, r=(), w=(), sem=None, inc=1, dma=False):
        deps = {}
        rec_eng = "dma" if dma else eng

        def add(tok):
            if tok is None:
                return
            k, v = tok
            if deps.get(k, 0) < v:
                deps[k] = v

        for (key, lo, hi) in r:
            psum = isinstance(key, tuple)
            for rec in self.recs.get(key, ()):
                if (rec[3] or (psum and rec[4] != rec_eng)) and rec[0] < hi and lo < rec[1]:
                    if rec[4] == rec_eng and rec_eng == "pe":
                        continue
                    add(rec[2])
        for (key, lo, hi) in w:
            for rec in self.recs.get(key, ()):
                if rec[0] < hi and lo < rec[1]:
                    if rec[4] == rec_eng and rec_eng == "pe":
                        continue
                    add(rec[2])
        waits = []
        wd = self.waited[eng]
        for k, v in deps.items():
            if wd.get(k, 0) < v:
                wd[k] = v
                waits.append((k, v))
        semkey = sem if sem is not None else eng
        self.cnt[semkey] = self.cnt.get(semkey, 0) + inc
        tok = (semkey, self.cnt[semkey])
        self.q[eng].append((waits, fn, semkey, inc))
        for (key, lo, hi) in w:
            lst = self.recs.setdefault(key, [])
            lst[:] = [rc for rc in lst if not (lo <= rc[0] and rc[1] <= hi)]
            lst.append((lo, hi, tok, True, rec_eng))
        for (key, lo, hi) in r:
            lst = self.recs.setdefault(key, [])
            lst[:] = [rc for rc in lst if not ((not rc[3]) and rc[4] == rec_eng and lo <= rc[0] and rc[1] <= hi)]
            lst.append((lo, hi, tok, False, rec_eng))
        return tok


class Buf:
    def __init__(self, arena, key, off, n):
        self.t, self.key, self.off, self.n = arena, key, off, n

    def ap(self, a=0, b=None, p=128, p0=0):
        b = self.n if b is None else b
        assert 0 <= a < b <= self.n, (a, b, self.n)
        return self.t[p0:p, self.off + a:self.off + b]

    def rg(self, a=0, b=None):
        b = self.n if b is None else b
        return (self.key, self.off + a, self.off + b)


class Arena:
    def __init__(self, t, key, size):
        self.t, self.key, self.size, self.pos = t, key, size, 0

    def alloc(self, n, at=None):
        if at is None:
            at = self.pos
            self.pos += n + (n % 2)
        assert at + n <= self.size, (self.key, at, n, self.size)
        return Buf(self.t, self.key, at, n)


def build_nc(n_tiles=NTILES):
    nc = bass.Bass("TRN2", target_bir_lowering=False)
    dram = {}

    def din(name, shape):
        dram[name] = nc.dram_tensor(name, list(shape), F32, kind="ExternalInput").ap()
        return dram[name]

    xT = din("xT", (D, TT))
    w_in = din("w_in", (D, 6656))
    w_grp = din("w_grp", (4, 128, 128))
    w_brh = din("w_brh", (D, D))
    w_brp = din("w_brp", (512, D))
    w_out = din("w_out", (D, D))
    w_fg = din("w_fg", (D, DFF))
    w_fu = din("w_fu", (D, DFF))
    w_fd = din("w_fd", (DFF, D))
    cf32 = din("cf32", (128, C_TOT))
    cfb32 = din("cfb32", (128, CB_TOT))
    prm = din("prm", (128, P_TOT))
    outT = nc.dram_tensor("outT", [D, SEQ], F32, kind="ExternalOutput").ap()

    FSZ = 26100
    BSZ = 27700
    from contextlib import ExitStack
    with ExitStack() as es:
        fa_t = es.enter_context(nc.sbuf_tensor("fa", [128, FSZ], F32))
        ba_t = es.enter_context(nc.sbuf_tensor("ba", [128, BSZ], BF16))
        wr_t = es.enter_context(nc.sbuf_tensor("wr", [128, NSLOT * SLOTW], BF16))
        ps = [es.enter_context(nc.psum_tensor(f"ps{i}", [128, 512], F32)) for i in range(7)]
        ps7 = es.enter_context(nc.psum_tensor("ps7", [128, 1024], BF16))
        semnames = ["pe", "act", "dve", "pool", "x0", "x1", "o0", "o1", "cst0", "cst1", "cst2"] + [f"w{i}" for i in range(NSLOT)]
        sems = {n: es.enter_context(nc.semaphore("s_" + n)) for n in semnames}

        FA = Arena(fa_t, "fa", FSZ)
        BA = Arena(ba_t, "ba", BSZ)
        P = Prog()

        def PS(i):
            return (("ps", i), 0, 1)

        H = [FA.alloc(8 * NTM), FA.alloc(8 * NTM)]
        CF = FA.alloc(C_TOT)
        PRM = FA.alloc(P_TOT)
        LBV = FA.alloc(8 * 4)
        GHH = FA.alloc(8)
        ZEROS = FA.alloc(128)
        HS = []
        for i in range(3):
            HS.append(dict(qs=FA.alloc(NTM), th=FA.alloc(NTM), kk=FA.alloc(NTM), eb=FA.alloc(NTM), ln=FA.alloc(NTM), cum=FA.alloc(NTM),
                           Qb=BA.alloc(NTM), Kb=BA.alloc(NTM), KbT=BA.alloc(512)))
        THG = [FA.alloc(NTM) for _ in range(4)]
        OSBUF = [FA.alloc(NTM), FA.alloc(NTM)]
        RS = [FA.alloc(NTM), FA.alloc(NTM)]
        CFB = Buf(fa_t, "fa", HS[0]["qs"].off, CB_TOT)
        PSCW = FA.alloc(4)
        LNV = FA.alloc(NTM)
        RSTD = FA.alloc(NTM)
        RSTD1 = FA.alloc(NTM)
        DUM = FA.alloc(2)
        XPW = 16 + 458
        SCR = FA.alloc(6 * XPW)
        GT = [Buf(fa_t, "fa", SCR.off + i * NTM, NTM) for i in (0, 1)]
        T2 = [FA.alloc(NTM), FA.alloc(NTM)]
        SG = [Buf(fa_t, "fa", SCR.off + i * NTM, NTM) for i in (2, 3)]
        XP = [Buf(fa_t, "fa", SCR.off + i * XPW, XPW) for i in range(4)]
        SA = Buf(fa_t, "fa", SCR.off + 4 * XPW, XPW)
        SB = Buf(fa_t, "fa", SCR.off + 5 * XPW, XPW)
        HALO = FA.alloc(64)
        TST = FA.alloc(8 * 128)
        DPREV = FA.alloc(8)

        U = BA.alloc(8 * NTM)
        shared0 = BA.pos
        V = BA.alloc(4 * 1024)
        GO = BA.alloc(8 * NTM)
        MIX = BA.alloc(8 * NTM)
        ACT = BA.alloc(NFC * NTM, at=shared0)
        assert shared0 + NFC * NTM <= BA.pos
        ATM = [BA.alloc(128) for _ in range(4)]
        S123 = [BA.alloc(3 * 128), BA.alloc(3 * 128)]
        SQ = [BA.alloc(NTM), BA.alloc(NTM)]
        POOLED = BA.alloc(4 * NTM)
        YP = BA.alloc(4 * NTM)
        SBF = [BA.alloc(8 * 128), BA.alloc(8 * 128)]
        IDENT = BA.alloc(128)
        ONESB = BA.alloc(128)

        SLOT = [Buf(wr_t, "wr", i * SLOTW, SLOTW) for i in range(NSLOT)]

        def act(fn, r, w):
            return P.emit("act", fn, r, w)

        def dve(fn, r, w):
            return P.emit("dve", fn, r, w)

        def pool(fn, r, w):
            return P.emit("pool", fn, r, w)

        phase = dict(name="")

        def pe(fn, r, w, lab=None):
            lab = lab or phase["name"]

            def fn2(e, fn=fn, lab=lab):
                ins = fn(e)
                if lab:
                    ins.annotate(lab)
                return ins
            return P.emit("pe", fn2, r, w)

        def wspec():
            g = []
            g.append(("WI0", w_in[:, 2048:2560], 8, 512))
            g.append(("WI1", w_in[:, 2560:3072], 8, 512))
            g.append(("WXP", w_in[:, 4096:4608], 8, 512))
            g.append(("WGRP", None, 4, 128))
            g.append(("WQ0", w_in[:, 0:512], 8, 512))
            g.append(("WF0", w_in[:, 1024:1536], 8, 512))
            g.append(("WOG0", w_in[:, 3072:3584], 8, 512))
            g.append(("WQ1", w_in[:, 512:1024], 8, 512))
            g.append(("WF1", w_in[:, 1536:2048], 8, 512))
            g.append(("WOG1", w_in[:, 3584:4096], 8, 512))
            for hf in range(2):
                g.append((f"WGB{hf}", w_in[:, 5632 + hf * 512:5632 + (hf + 1) * 512], 8, 512))
                g.append((f"WBP{hf}", w_brp[:, hf * 512:(hf + 1) * 512], 4, 512))
            for hf in range(2):
                g.append((f"WGA{hf}", w_in[:, 4608 + hf * 512:4608 + (hf + 1) * 512], 8, 512))
                g.append((f"WBH{hf}", w_brh[:, hf * 512:(hf + 1) * 512], 8, 512))
            g.append(("WO0", w_out[:, 0:512], 8, 512))
            g.append(("WO1", w_out[:, 512:1024], 8, 512))
            for i in range(6):
                nc_ = 512 if i < 5 else 256
                g.append((f"WG{i}", w_fg[:, i * 512:i * 512 + nc_], 8, nc_))
                g.append((f"WU{i}", w_fu[:, i * 512:i * 512 + nc_], 8, nc_))
            for j in range(8):
                g.append((f"WD{j}", w_fd[:, j * 128:(j + 1) * 128], NFC, 128))
            return g

        WS = wspec()
        NG = len(WS)
        GIDX = {sp[0]: i for i, sp in enumerate(WS)}
        wst = dict(next_load=0, free=list(range(NSLOT)), loaded={})

        def issue_load(gi, slot):
            ti, g = divmod(gi, NG)
            name, src, nkc, ncols = WS[g]
            sb = SLOT[slot]
            n = nkc * ncols
            assert n <= SLOTW
            if name == "WGRP":
                src_ap = w_grp.rearrange("g c d -> c g d")
            else:
                src_ap = src.rearrange("(kc p) c -> p kc c", p=128)
            dst_ap = sb.ap(0, n).rearrange("p (kc c) -> p kc c", kc=nkc)

            def fn(e, dst_ap=dst_ap, src_ap=src_ap):
                return e.dma_start(out=dst_ap, in_=src_ap)
            P.emit("pool", fn, r=[], w=[sb.rg(0, n)], sem=f"w{slot}", inc=16, dma=True)

        def pump():
            while wst["next_load"] < n_tiles * NG and wst["free"]:
                slot = wst["free"].pop(0)
                issue_load(wst["next_load"], slot)
                wst["loaded"][wst["next_load"]] = slot
                wst["next_load"] += 1

        def need(ti, name):
            g = GIDX[name]
            gi = ti * NG + g
            pump()
            assert gi in wst["loaded"], ("weight group not resident (ring too small / order)", ti, name)
            return SLOT[wst["loaded"][gi]], WS[g][3]

        def rel(ti, name):
            gi = ti * NG + GIDX[name]
            wst["free"].append(wst["loaded"].pop(gi))
            pump()

        def prologue():
            P.emit("sp", lambda e: e.dma_start(out=CF.ap(), in_=cf32[:, :]), r=[], w=[CF.rg()], sem="cst0", inc=16, dma=True)
            P.emit("sp", lambda e: e.dma_start(out=PRM.ap(), in_=prm[:, :]), r=[], w=[PRM.rg()], sem="cst1", inc=16, dma=True)
            P.emit("sp", lambda e: e.dma_start(out=CFB.ap(), in_=cfb32[:, :]), r=[], w=[CFB.rg()], sem="cst2", inc=16, dma=True)
            dve(lambda e: e.tensor_copy(out=IDENT.ap(), in_=CFB.ap(CB_IDENT, CB_IDENT + 128)), [CFB.rg()], [IDENT.rg()])
            dve(lambda e: e.tensor_copy(out=ONESB.ap(), in_=CFB.ap(CB_ONES, CB_ONES + 128)), [CFB.rg()], [ONESB.rg()])
            dve(lambda e: e.memset(ZEROS.ap(), 0.0), [], [ZEROS.rg()])
            dve(lambda e: e.memset(TST.ap(), 0.0), [], [TST.rg()])
            dve(lambda e: e.memset(SBF[0].ap(), 0.0), [], [SBF[0].rg()])
            dve(lambda e: e.memset(DPREV.ap(), 0.0), [], [DPREV.rg()])
            for g in range(4):
                dve(lambda e, g=g: e.tensor_scalar(out=PSCW.ap(g, g + 1), in0=PRM.ap(P_PSC + g, P_PSC + g + 1), scalar1=1.0 / (2 << g),
                                                   scalar2=None, op0=ALU.mult),
                    [PRM.rg()], [PSCW.rg(g, g + 1)])
            dve(lambda e: e.memset(HALO.ap(), 0.0), [], [HALO.rg()])
            dve(lambda e: e.tensor_tensor(out=LBV.ap(24, 32), in0=PRM.ap(P_L0, P_L0 + 8), in1=PRM.ap(P_L1, P_L1 + 8), op=ALU.subtract),
                [PRM.rg()], [LBV.rg(24, 32)])
            act(lambda e: e.activation(out=LBV.ap(24, 32), in_=LBV.ap(24, 32), func=AF.Tanh, scale=0.5), [LBV.rg(24, 32)], [LBV.rg(24, 32)])
            dve(lambda e: e.tensor_scalar(out=LBV.ap(0, 8), in0=LBV.ap(24, 32), scalar1=0.25, scalar2=0.75, op0=ALU.mult, op1=ALU.add),
                [LBV.rg(24, 32)], [LBV.rg(0, 8)])
            dve(lambda e: e.tensor_scalar(out=LBV.ap(8, 16), in0=LBV.ap(24, 32), scalar1=-0.25, scalar2=0.25, op0=ALU.mult, op1=ALU.add),
                [LBV.rg(24, 32)], [LBV.rg(8, 16)])
            dve(lambda e: e.tensor_scalar(out=LBV.ap(16, 24), in0=LBV.ap(24, 32), scalar1=0.25, scalar2=-0.25, op0=ALU.mult, op1=ALU.add),
                [LBV.rg(24, 32)], [LBV.rg(16, 24)])
            dve(lambda e: e.tensor_scalar(out=GHH.ap(), in0=PRM.ap(P_GHG, P_GHG + 8), scalar1=0.5, scalar2=None, op0=ALU.mult),
                [PRM.rg()], [GHH.rg()])

        def load_x(ti):
            nt, t0 = SIZES[ti], OFFS[ti]
            hb = H[ti % 2]
            dst = hb.ap().rearrange("p (kc n) -> p kc n", kc=8)[:, :, 0:nt]
            src = xT[:, t0:t0 + nt].rearrange("(kc p) n -> p kc n", p=128)
            P.emit("sp", lambda e: e.dma_start(out=dst, in_=src), r=[], w=[hb.rg()], sem=f"x{ti % 2}", inc=16, dma=True)

        def store_out(ti):
            nt, t0 = SIZES[ti], OFFS[ti]
            hb = H[ti % 2]
            n0 = NMETA if ti == 0 else 0
            src = hb.ap().rearrange("p (kc n) -> p kc n", kc=8)[:, :, n0:nt]
            dst = outT[:, t0 + n0 - NMETA:t0 + nt - NMETA].rearrange("(kc p) n -> p kc n", p=128)
            P.emit("sp", lambda e: e.dma_start(out=dst, in_=src), r=[hb.rg()], w=[], sem=f"o{ti % 2}", inc=16, dma=True)

        def norm_sq(src, nt, kc):
            sq = SQ[kc % 2]
            act(lambda e, sq=sq, kc=kc: e.activation(out=sq.ap(0, nt), in_=src.ap(kc * NTM, kc * NTM + nt), func=AF.Square),
                [src.rg(kc * NTM, kc * NTM + nt)], [sq.rg(0, nt)])
            pe(lambda e, sq=sq, kc=kc: e.matmul(ps[6][:, 0:nt], lhsT=ONESB.ap(), rhs=sq.ap(0, nt), start=(kc == 0), stop=(kc == 7)),
               [ONESB.rg(), sq.rg(0, nt)], [PS(6)])

        def norm_fin(nt, rstd):
            act(lambda e: e.activation(out=LNV.ap(0, nt), in_=ps[6][:, 0:nt], func=AF.Ln, scale=1.0 / D, bias=EPS),
                [PS(6)], [LNV.rg(0, nt)])
            act(lambda e: e.activation(out=rstd.ap(0, nt), in_=LNV.ap(0, nt), func=AF.Exp, scale=-0.5),
                [LNV.rg(0, nt)], [rstd.rg(0, nt)])

        def norm_apply(src, nt, gcol, dst, rstd):
            for kc in range(8):
                dve(lambda e, kc=kc: e.scalar_tensor_tensor(out=dst.ap(kc * NTM, kc * NTM + nt), in0=src.ap(kc * NTM, kc * NTM + nt),
                                                            scalar=PRM.ap(gcol + kc, gcol + kc + 1), in1=rstd.ap(0, nt),
                                                            op0=ALU.mult, op1=ALU.mult),
                    [src.rg(kc * NTM, kc * NTM + nt), PRM.rg(), rstd.rg(0, nt)], [dst.rg(kc * NTM, kc * NTM + nt)])

        def rmsnorm(src, nt, gcol, dst, rstd=None):
            rstd = rstd or RSTD
            for kc in range(8):
                norm_sq(src, nt, kc)
            norm_fin(nt, rstd)
            norm_apply(src, nt, gcol, dst, rstd)

        def table_preswitch_ln():
            act(lambda e: e.activation(out=DUM.ap(0, 1), in_=ZEROS.ap(0, 1), func=AF.Ln, bias=1.0), [ZEROS.rg(0, 1)], [DUM.rg(0, 1)])

        def proj(bank, wslot, ncols, col, xbuf, nt, nkc=8, xstride=NTM, lab=None, split=False):
            if split:
                for kc in range(nkc):
                    pe(lambda e, kc=kc: e.matmul(ps[bank][:, 0:nt], lhsT=wslot.ap(kc * ncols + col, kc * ncols + col + 128),
                                                 rhs=xbuf.ap(kc * xstride, kc * xstride + nt), start=(kc == 0), stop=(kc == nkc - 1)),
                       [wslot.rg(0, nkc * ncols), xbuf.rg(kc * xstride, kc * xstride + nt)], [PS(bank)], lab=lab)
                return

            def fn(e):
                last = None
                for kc in range(nkc):
                    last = e.matmul(ps[bank][:, 0:nt], lhsT=wslot.ap(kc * ncols + col, kc * ncols + col + 128),
                                    rhs=xbuf.ap(kc * xstride, kc * xstride + nt), start=(kc == 0), stop=(kc == nkc - 1))
                return last
            pe(fn, [wslot.rg(0, nkc * ncols), xbuf.rg(0, (nkc - 1) * xstride + nt)], [PS(bank)], lab=lab)

        R6 = [0, 1, 2, 3, 4, 5]
        R3 = [0, 1, 2]
        ring = dict(i=0)

        held = set()

        def rbank(banks):
            for _ in range(len(banks)):
                b = banks[ring["i"] % len(banks)]
                ring["i"] += 1
                if b not in held:
                    held.add(b)
                    return b
            raise AssertionError(("no free PSUM bank", banks, sorted(held)))

        def rfree(*bs):
            for b in bs:
                held.discard(b)

        def gbank(banks):
            while all(b in held for b in banks):
                yield "blocked"
            return rbank(banks)

        def tile_prog(ti):
            nt, t0 = SIZES[ti], OFFS[ti]
            hb = H[ti % 2]
            chunks = chunks_of(nt)
            cmax = max(c for _, c in chunks)
            if ti + 1 < n_tiles:
                load_x(ti + 1)
            phase['name'] = f't{ti}.V'
            flip = 0
            for hf in range(2):
                wsl, _ = need(ti, f"WI{hf}")
                for ci, (c0, c) in enumerate(chunks):
                    b = rbank([0, 1, 2, 3, 4, 5])

                    def fn(e, b=b, c0=c0, c=c, wsl=wsl):
                        last = None
                        for kc in range(8):
                            last = e.matmul(ps[b][0:c, 0:512], lhsT=U.ap(kc * NTM + c0, kc * NTM + c0 + c),
                                            rhs=wsl.ap(kc * 512, (kc + 1) * 512), start=(kc == 0), stop=(kc == 7))
                        return last
                    pe(fn, [U.rg(), wsl.rg()], [PS(b)])
                    vo = ci * 1024 + hf * 512
                    if flip % 2 == 0:
                        act(lambda e, b=b, c=c, vo=vo: e.activation(out=V.ap(vo, vo + 512, p=c), in_=ps[b][0:c, 0:512], func=AF.Copy),
                            [PS(b)], [V.rg(vo, vo + 512)])
                    else:
                        dve(lambda e, b=b, c=c, vo=vo: e.tensor_copy(out=V.ap(vo, vo + 512, p=c), in_=ps[b][0:c, 0:512]),
                            [PS(b)], [V.rg(vo, vo + 512)])
                    rfree(b)
                    flip += 1
                rel(ti, f"WI{hf}")

            def pool_mixer_1():
                lab = f't{ti}.poolmix'
                wxp, _ = need(ti, "WXP")
                for g in range(4):
                    w = 2 << g
                    b = rbank(R6)
                    proj(b, wxp, 512, g * 128, U, nt, lab=lab)
                    xp = XP[g]
                    act(lambda e, b=b, xp=xp: e.activation(out=xp.ap(16, 16 + nt), in_=ps[b][:, 0:nt], func=AF.Copy),
                        [PS(b)], [xp.rg(16, 16 + nt)])
                    rfree(b)
                    pool(lambda e, xp=xp, g=g: e.tensor_copy(out=xp.ap(0, 16), in_=HALO.ap(g * 16, (g + 1) * 16)),
                         [HALO.rg(g * 16, (g + 1) * 16)], [xp.rg(0, 16)])
                    end = 16 + nt
                    srcb = xp
                    bufs = [SA, SB]
                    k = 0
                    sh = 1
                    lo = 1
                    while sh < w:
                        dstb = bufs[k % 2]
                        pool(lambda e, srcb=srcb, dstb=dstb, lo=lo, sh=sh, end=end: e.tensor_tensor(
                            out=dstb.ap(lo, end), in0=srcb.ap(lo, end), in1=srcb.ap(lo - sh, end - sh), op=ALU.add),
                            [srcb.rg(lo - sh, end)], [dstb.rg(lo, end)])
                        srcb = dstb
                        k += 1
                        sh *= 2
                        lo += sh
                    sw = srcb
                    dve(lambda e, sw=sw, xp=xp, g=g, w=w, end=end: e.scalar_tensor_tensor(
                        out=POOLED.ap(g * NTM, g * NTM + nt), in0=xp.ap(16, end), scalar=-float(w), in1=sw.ap(16, end),
                        op0=ALU.mult, op1=ALU.add),
                        [xp.rg(16, end), sw.rg(16, end)], [POOLED.rg(g * NTM, g * NTM + nt)])
                    if ti == 0:
                        tmp = RS[0]
                        dve(lambda e, sw=sw, tmp=tmp, g=g: e.tensor_tensor(out=tmp.ap(0, 16), in0=sw.ap(16, 32),
                                                                          in1=CF.ap(C_WC + g * 16, C_WC + (g + 1) * 16), op=ALU.mult),
                            [sw.rg(16, 32), CF.rg()], [tmp.rg(0, 16)])
                        dve(lambda e, tmp=tmp, xp=xp, g=g, w=w: e.scalar_tensor_tensor(
                            out=POOLED.ap(g * NTM, g * NTM + 16), in0=xp.ap(16, 32), scalar=-float(w), in1=tmp.ap(0, 16),
                            op0=ALU.mult, op1=ALU.add),
                            [xp.rg(16, 32), tmp.rg(0, 16)], [POOLED.rg(g * NTM, g * NTM + 16)])
                    pool(lambda e, xp=xp, g=g: e.tensor_copy(out=HALO.ap(g * 16, (g + 1) * 16), in_=xp.ap(nt, nt + 16)),
                         [xp.rg(nt, nt + 16)], [HALO.rg(g * 16, (g + 1) * 16)])
                rel(ti, "WXP")

            def pool_mixer_2():
                lab = f't{ti}.poolmix2'
                wgrp, _ = need(ti, "WGRP")
                for g in range(4):
                    b2 = rbank(R3)
                    pe(lambda e, b2=b2, g=g: e.matmul(ps[b2][:, 0:nt], lhsT=wgrp.ap(g * 128, (g + 1) * 128),
                                                      rhs=POOLED.ap(g * NTM, g * NTM + nt), start=True, stop=True),
                       [wgrp.rg(0, 512), POOLED.rg(g * NTM, g * NTM + nt)], [PS(b2)], lab=lab)
                    act(lambda e, b2=b2, g=g: e.activation(out=YP.ap(g * NTM, g * NTM + nt), in_=ps[b2][:, 0:nt], func=AF.Identity,
                                                           scale=PSCW.ap(g, g + 1)),
                        [PS(b2), PSCW.rg()], [YP.rg(g * NTM, g * NTM + nt)])
                    rfree(b2)
                rel(ti, "WGRP")

            done = set()

            def wait_for(kind, h):
                while h >= 0 and (kind, h) not in done:
                    yield "blocked"

            def gen_a(h):
                lab = f't{ti}.A{h}'
                yield from wait_for("A", h - 3)
                yield from wait_for("C", h - 4)
                hf, hc = divmod(h, 4)
                hc *= 128
                wq, _ = need(ti, f"WQ{hf}")
                wf, _ = need(ti, f"WF{hf}")
                wog, _ = need(ti, f"WOG{hf}")
                s = HS[h % 3]
                thg = THG[h % 4]
                qb = yield from gbank(R3)
                proj(qb, wq, 512, hc, U, nt, lab=lab)
                yield
                fb = yield from gbank(R3)
                proj(fb, wf, 512, hc, U, nt, lab=lab)
                yield
                gb = yield from gbank(R3)
                proj(gb, wog, 512, hc, U, nt, lab=lab)
                yield
                act(lambda e: e.activation(out=s["qs"].ap(0, nt), in_=ps[qb][:, 0:nt], func=AF.Silu), [PS(qb)], [s["qs"].rg(0, nt)])
                act(lambda e: e.activation(out=s["th"].ap(0, nt), in_=ps[fb][:, 0:nt], func=AF.Tanh, scale=0.5), [PS(fb)], [s["th"].rg(0, nt)])
                act(lambda e: e.activation(out=thg.ap(0, nt), in_=ps[gb][:, 0:nt], func=AF.Tanh, scale=0.5), [PS(gb)], [thg.rg(0, nt)])
                rfree(qb, fb, gb)
                yield "mid"
                act(lambda e: e.activation(out=s["ln"].ap(0, nt), in_=s["th"].ap(0, nt), func=AF.Ln, scale=LBV.ap(8 + h, 9 + h),
                                           bias=LBV.ap(h, h + 1)),
                    [s["th"].rg(0, nt), LBV.rg()], [s["ln"].rg(0, nt)])
                dve(lambda e: e.tensor_scalar(out=s["kk"].ap(0, nt), in0=s["th"].ap(0, nt), scalar1=LBV.ap(16 + h, 17 + h),
                                              scalar2=LBV.ap(8 + h, 9 + h), op0=ALU.mult, op1=ALU.add),
                    [s["th"].rg(0, nt), LBV.rg()], [s["kk"].rg(0, nt)])
                yield
                yield from wait_for("B", h - 3)
                for (c0, c) in chunks:
                    dve(lambda e, c0=c0, c=c: e.tensor_tensor_scan(out=s["cum"].ap(c0, c0 + c), data0=s["ln"].ap(c0, c0 + c),
                                                                   data1=ZEROS.ap(0, c), initial=0.0, op0=ALU.add, op1=ALU.add),
                        [s["ln"].rg(c0, c0 + c), ZEROS.rg()], [s["cum"].rg(c0, c0 + c)])
                    yield
                act(lambda e: e.activation(out=s["eb"].ap(0, nt), in_=s["cum"].ap(0, nt), func=AF.Exp), [s["cum"].rg(0, nt)], [s["eb"].rg(0, nt)])
                act(lambda e: e.activation(out=s["th"].ap(0, nt), in_=s["cum"].ap(0, nt), func=AF.Exp, scale=-1.0),
                    [s["cum"].rg(0, nt)], [s["th"].rg(0, nt)])
                yield
                pool(lambda e: e.tensor_tensor(out=s["Qb"].ap(0, nt), in0=s["qs"].ap(0, nt), in1=s["eb"].ap(0, nt), op=ALU.mult),
                     [s["qs"].rg(0, nt), s["eb"].rg(0, nt)], [s["Qb"].rg(0, nt)])
                dve(lambda e: e.tensor_tensor(out=s["Kb"].ap(0, nt), in0=s["kk"].ap(0, nt), in1=s["th"].ap(0, nt), op=ALU.mult),
                    [s["kk"].rg(0, nt), s["th"].rg(0, nt)], [s["Kb"].rg(0, nt)])
                yield

                def tr(e):
                    last = None
                    for ci, (c0, c) in enumerate(chunks):
                        last = e.transpose(out=ps7[0:c, ci * 128:(ci + 1) * 128], in_=s["Kb"].ap(c0, c0 + c), identity=IDENT.ap())
                    return last
                pe(tr, [s["Kb"].rg(0, nt), IDENT.rg()], [PS(7)], lab=lab)
                ci = 0
                while ci < 4:
                    cj = ci
                    while cj + 1 < 4 and chunks[cj + 1][1] == chunks[ci][1]:
                        cj += 1
                    c = chunks[ci][1]
                    dve(lambda e, ci=ci, cj=cj, c=c: e.tensor_copy(out=s["KbT"].ap(ci * 128, (cj + 1) * 128, p=c),
                                                                  in_=ps7[0:c, ci * 128:(cj + 1) * 128]),
                        [PS(7)], [s["KbT"].rg(ci * 128, (cj + 1) * 128)])
                    ci = cj + 1
                if h % 4 == 3:
                    rel(ti, f"WQ{hf}")
                    rel(ti, f"WF{hf}")
                    rel(ti, f"WOG{hf}")
                yield

            def gen_b(h):
                lab = f't{ti}.B{h}'
                yield from wait_for("B", h - 1)
                s = HS[h % 3]
                hcol = h * 128
                st = S123[h % 2]
                s_in = SBF[ti % 2]
                s_out = SBF[(ti + 1) % 2]
                S_ap = [s_in.ap(hcol, hcol + 128)] + [st.ap(i * 128, (i + 1) * 128) for i in range(3)]
                S_rg = [s_in.rg(hcol, hcol + 128)] + [st.rg(i * 128, (i + 1) * 128) for i in range(3)]
                Sn_ap = [st.ap(i * 128, (i + 1) * 128) for i in range(3)] + [s_out.ap(hcol, hcol + 128)]
                Sn_rg = [st.rg(i * 128, (i + 1) * 128) for i in range(3)] + [s_out.rg(hcol, hcol + 128)]
                Tb = (TST.ap(hcol, hcol + 128), TST.rg(hcol, hcol + 128))

                def at_all(e):
                    last = None
                    for ci, (c0, c) in enumerate(chunks):
                        last = e.matmul(ps[4][0:c, ci * 128:ci * 128 + c], lhsT=s["Kb"].ap(c0, c0 + c), rhs=s["Qb"].ap(c0, c0 + c),
                                        start=True, stop=True)
                    return last
                pe(at_all, [s["Kb"].rg(0, nt), s["Qb"].rg(0, nt)], [PS(4)], lab=lab)

                def p_all(e):
                    last = None
                    for ci, (c0, c) in enumerate(chunks):
                        vo = ci * 1024 + hcol
                        last = e.matmul(ps[5][:, ci * 128:(ci + 1) * 128], lhsT=s["KbT"].ap(ci * 128, (ci + 1) * 128, p=c),
                                        rhs=V.ap(vo, vo + 128, p=c), start=True, stop=True)
                    return last
                pe(p_all, [s["KbT"].rg(), V.rg()], [PS(5)], lab=lab)
                yield
                for ci, (c0, c) in enumerate(chunks):
                    am = ATM[ci]
                    dve(lambda e, c=c, am=am, ci=ci: e.tensor_tensor(out=am.ap(0, c, p=c), in0=ps[4][0:c, ci * 128:ci * 128 + c],
                                                                    in1=CF.ap(C_MASK, C_MASK + c, p=c), op=ALU.mult),
                        [PS(4), CF.rg()], [am.rg()])
                for ci, (c0, c) in enumerate(chunks):
                    if ci == 0:
                        dec_ap, dec_rg = DPREV.ap(h, h + 1), DPREV.rg(h, h + 1)
                    else:
                        pc0, pc = chunks[ci - 1]
                        dec_ap, dec_rg = s["eb"].ap(pc0 + pc - 1, pc0 + pc), s["eb"].rg(pc0 + pc - 1, pc0 + pc)
                    dve(lambda e, dec_ap=dec_ap, ci=ci: e.scalar_tensor_tensor(out=Tb[0], in0=Tb[0], scalar=dec_ap, in1=ps[5][:, ci * 128:(ci + 1) * 128],
                                                                               op0=ALU.mult, op1=ALU.add),
                        [Tb[1], dec_rg, PS(5)], [Tb[1]])
                    dve(lambda e, c0=c0, c=c, ci=ci: e.tensor_scalar(out=Sn_ap[ci], in0=Tb[0], scalar1=s["eb"].ap(c0 + c - 1, c0 + c), scalar2=None,
                                                                     op0=ALU.mult),
                        [Tb[1], s["eb"].rg(c0 + c - 1, c0 + c)], [Sn_rg[ci]])
                yield

                def o_all(e):
                    last = None
                    for ci, (c0, c) in enumerate(chunks):
                        vo = ci * 1024 + hcol
                        e.matmul(ps[3][:, c0:c0 + c], lhsT=V.ap(vo, vo + 128, p=c), rhs=ATM[ci].ap(0, c, p=c), start=True, stop=False)
                        last = e.matmul(ps[3][:, c0:c0 + c], lhsT=S_ap[ci], rhs=s["Qb"].ap(c0, c0 + c), start=False, stop=True)
                    return last
                pe(o_all, [V.rg(), s["Qb"].rg(0, nt)] + [a_.rg() for a_ in ATM] + S_rg, [PS(3)], lab=lab)
                yield
                osb = OSBUF[h % 2]
                sq = SQ[h % 2]
                yield from wait_for("C", h - 2)
                dve(lambda e: e.tensor_copy(out=osb.ap(0, nt), in_=ps[3][:, 0:nt]), [PS(3)], [osb.rg(0, nt)])
                act(lambda e: e.activation(out=sq.ap(0, nt), in_=ps[3][:, 0:nt], func=AF.Square), [PS(3)], [sq.rg(0, nt)])
                dve(lambda e: e.tensor_copy(out=DPREV.ap(h, h + 1), in_=s["eb"].ap(nt - 1, nt)), [s["eb"].rg(nt - 1, nt)], [DPREV.rg(h, h + 1)])
                yield

            def gen_c(h):
                lab = f't{ti}.C{h}'
                yield from wait_for("C", h - 1)
                osb = OSBUF[h % 2]
                sq = SQ[h % 2]
                rs = RS[h % 2]
                t2 = T2[h % 2]
                thg = THG[h % 4]
                pe(lambda e: e.matmul(ps[6][:, 0:nt], lhsT=ONESB.ap(), rhs=sq.ap(0, nt), start=True, stop=True),
                   [ONESB.rg(), sq.rg(0, nt)], [PS(6)], lab=lab)
                yield
                act(lambda e: e.activation(out=rs.ap(0, nt), in_=ps[6][:, 0:nt], func=AF.Ln, scale=1.0 / 128, bias=EPS), [PS(6)], [rs.rg(0, nt)])
                act(lambda e: e.activation(out=rs.ap(0, nt), in_=rs.ap(0, nt), func=AF.Exp, scale=-0.5), [rs.rg(0, nt)], [rs.rg(0, nt)])
                yield
                dve(lambda e: e.scalar_tensor_tensor(out=t2.ap(0, nt), in0=osb.ap(0, nt), scalar=GHH.ap(h, h + 1), in1=rs.ap(0, nt),
                                                     op0=ALU.mult, op1=ALU.mult),
                    [osb.rg(0, nt), GHH.rg(), rs.rg(0, nt)], [t2.rg(0, nt)])
                yield
                dve(lambda e: e.scalar_tensor_tensor(out=GO.ap(h * NTM, h * NTM + nt), in0=thg.ap(0, nt), scalar=1.0, in1=t2.ap(0, nt),
                                                     op0=ALU.add, op1=ALU.mult),
                    [thg.rg(0, nt), t2.rg(0, nt)], [GO.rg(h * NTM, h * NTM + nt)])
                yield

            def gen_m(j0, j1):
                lab = f't{ti}.M'
                for j in range(j0, j1):
                    hf, jc = divmod(j, 4)
                    jc *= 128
                    wgb, _ = need(ti, f"WGB{hf}")
                    wbp, _ = need(ti, f"WBP{hf}")
                    b = yield from gbank(R3)
                    proj(b, wgb, 512, jc, U, nt, lab=lab)
                    yield
                    gb2 = GT[1]
                    act(lambda e, b=b, gb2=gb2: e.activation(out=gb2.ap(0, nt), in_=ps[b][:, 0:nt], func=AF.Tanh, scale=0.5), [PS(b)], [gb2.rg(0, nt)])
                    rfree(b)
                    b2 = yield from gbank(R3)
                    proj(b2, wbp, 512, jc, YP, nt, nkc=4, lab=lab)
                    yield
                    dve(lambda e, gb2=gb2, b2=b2, j=j: e.scalar_tensor_tensor(out=MIX.ap(j * NTM, j * NTM + nt), in0=gb2.ap(0, nt), scalar=1.0,
                                                                            in1=ps[b2][:, 0:nt], op0=ALU.add, op1=ALU.mult),
                        [gb2.rg(0, nt), PS(b2)], [MIX.rg(j * NTM, j * NTM + nt)])
                    rfree(b2)
                    if j % 4 == 3:
                        rel(ti, f"WGB{hf}")
                        rel(ti, f"WBP{hf}")
                    yield

            pool_mixer_1()
            active = []

            def start(kind, h):
                if kind == "M":
                    g_ = gen_m(0, 4)
                else:
                    g_ = {"A": gen_a, "B": gen_b, "C": gen_c}[kind](h)
                active.append((kind, h, g_))

            start("A", 0)
            PRIO = {"B": 0, "C": 1, "A": 2, "M": 3}
            while active:
                progressed = False
                for item in sorted(active, key=lambda it: (PRIO[it[0]], it[1])):
                    kind, h, g_ = item
                    try:
                        sig = next(g_)
                        if sig != "blocked":
                            progressed = True
                    except StopIteration:
                        progressed = True
                        active.remove(item)
                        done.add((kind, h))
                        if kind == "A":
                            start("B", h)
                            if h == 1:
                                pool_mixer_2()
                            if h == 4:
                                start("M", 0)
                        elif kind == "B":
                            start("C", h)
                        continue
                    if sig == "mid" and kind == "A" and h + 1 < 8:
                        start("A", h + 1)
                        progressed = True
                if not progressed:
                    raise AssertionError(("HGRN scheduler stuck", [(k_, h_) for k_, h_, _ in active]))

            phase['name'] = f't{ti}.merge'
            for _ in gen_m(4, 8):
                pass
            for j in range(8):
                hf, jc = divmod(j, 4)
                jc *= 128
                wga, _ = need(ti, f"WGA{hf}")
                wbh, _ = need(ti, f"WBH{hf}")
                bga = rbank(R6)
                proj(bga, wga, 512, jc, U, nt)
                bya = rbank(R6)
                proj(bya, wbh, 512, jc, GO, nt)
                ga = GT[0]
                m1 = T2[j % 2]
                act(lambda e, bga=bga, ga=ga: e.activation(out=ga.ap(0, nt), in_=ps[bga][:, 0:nt], func=AF.Tanh, scale=0.5), [PS(bga)], [ga.rg(0, nt)])
                dve(lambda e, ga=ga, m1=m1, bya=bya: e.scalar_tensor_tensor(out=m1.ap(0, nt), in0=ga.ap(0, nt), scalar=1.0, in1=ps[bya][:, 0:nt],
                                                                            op0=ALU.add, op1=ALU.mult),
                    [ga.rg(0, nt), PS(bya)], [m1.rg(0, nt)])
                rfree(bga, bya)
                pool(lambda e, m1=m1, j=j: e.tensor_tensor(out=MIX.ap(j * NTM, j * NTM + nt), in0=m1.ap(0, nt), in1=MIX.ap(j * NTM, j * NTM + nt), op=ALU.add),
                     [m1.rg(0, nt), MIX.rg(j * NTM, j * NTM + nt)], [MIX.rg(j * NTM, j * NTM + nt)])
                if j % 4 == 3:
                    rel(ti, f"WGA{hf}")
                    rel(ti, f"WBH{hf}")

            table_preswitch_ln()
            phase['name'] = f't{ti}.oproj'
            for j in range(8):
                hf, jc = divmod(j, 4)
                jc *= 128
                wo, _ = need(ti, f"WO{hf}")
                b = rbank([0, 1, 2, 3, 4, 5])
                proj(b, wo, 512, jc, MIX, nt)
                dve(lambda e, b=b, j=j: e.scalar_tensor_tensor(out=hb.ap(j * NTM, j * NTM + nt), in0=ps[b][:, 0:nt], scalar=0.5,
                                                               in1=hb.ap(j * NTM, j * NTM + nt), op0=ALU.mult, op1=ALU.add),
                    [PS(b), hb.rg(j * NTM, j * NTM + nt)], [hb.rg(j * NTM, j * NTM + nt)])
                rfree(b)
                if j % 4 == 3:
                    rel(ti, f"WO{hf}")
                if j >= 1:
                    norm_sq(hb, nt, j - 1)
            norm_sq(hb, nt, 7)

            phase['name'] = f't{ti}.ffn'
            norm_fin(nt, RSTD)
            norm_apply(hb, nt, P_G2, U, RSTD)
            for fc in range(NFC):
                g, col = divmod(fc, 4)
                col *= 128
                wg, ncg = need(ti, f"WG{g}")
                wu, ncu = need(ti, f"WU{g}")
                bg = rbank([0, 1, 2, 3, 4, 5])
                proj(bg, wg, ncg, col, U, nt, split=(fc == 0))
                bu = rbank([0, 1, 2, 3, 4, 5])
                proj(bu, wu, ncu, col, U, nt, split=(fc == 0))
                sg = SG[fc % 2]
                act(lambda e, bg=bg, sg=sg: e.activation(out=sg.ap(0, nt), in_=ps[bg][:, 0:nt], func=AF.Silu), [PS(bg)], [sg.rg(0, nt)])
                dve(lambda e, bu=bu, sg=sg, fc=fc: e.tensor_tensor(out=ACT.ap(fc * NTM, fc * NTM + nt), in0=ps[bu][:, 0:nt], in1=sg.ap(0, nt), op=ALU.mult),
                    [PS(bu), sg.rg(0, nt)], [ACT.rg(fc * NTM, fc * NTM + nt)])
                rfree(bg, bu)
                if fc % 4 == 3 or fc == NFC - 1:
                    rel(ti, f"WG{g}")
                    rel(ti, f"WU{g}")
            if ti + 1 < n_tiles:
                phase['name'] = f't{ti + 1}.norm1'
                rmsnorm(H[(ti + 1) % 2], SIZES[ti + 1], P_G1, U, rstd=RSTD1)
            phase['name'] = f't{ti}.down'
            for j in range(8):
                wd, _ = need(ti, f"WD{j}")
                b = rbank([0, 1, 2, 3, 4, 5])
                proj(b, wd, 128, 0, ACT, nt, nkc=NFC)
                rel(ti, f"WD{j}")
                dve(lambda e, b=b, j=j: e.tensor_tensor(out=hb.ap(j * NTM, j * NTM + nt), in0=ps[b][:, 0:nt], in1=hb.ap(j * NTM, j * NTM + nt), op=ALU.add),
                    [PS(b), hb.rg(j * NTM, j * NTM + nt)], [hb.rg(j * NTM, j * NTM + nt)])
                rfree(b)
            phase['name'] = f't{ti}.fin'
            rmsnorm(hb, nt, P_G3, hb)
            store_out(ti)

        prologue()
        load_x(0)
        phase['name'] = 't0.norm1'
        rmsnorm(H[0], SIZES[0], P_G1, U, rstd=RSTD1)
        for ti in range(n_tiles):
            tile_prog(ti)
        fin = []
        for k in ("o0", "o1"):
            if P.cnt.get(k, 0) > 0:
                fin.append((k, P.cnt[k]))

        def replay(engname, e):
            semE = sems
            for (waits, fn, semkey, inc) in P.q[engname]:
                for (k, v) in waits:
                    e.wait_ge(semE[k], v)
                ins = fn(e)
                ins.then_inc(semE[semkey], inc)

        with nc.Block() as block:
            @block.tensor
            def _(e):
                replay("pe", e)

            @block.scalar
            def _(e):
                replay("act", e)

            @block.vector
            def _(e):
                replay("dve", e)

            @block.gpsimd
            def _(e):
                replay("pool", e)

            @block.sync
            def _(e):
                replay("sp", e)
                for (k, v) in fin:
                    e.wait_ge(sems[k], v)
    return nc


def make_consts():
    cf = np.zeros((128, C_TOT), np.float32)
    cfb = np.zeros((128, CB_TOT), np.float32)
    cfb[:, CB_IDENT:CB_IDENT + 128] = np.eye(128, dtype=np.float32)
    cfb[:, CB_ONES:CB_ONES + 128] = 1.0
    s = np.arange(128)[:, None]
    t = np.arange(128)[None, :]
    cf[:, C_MASK:C_MASK + 128] = (s <= t).astype(np.float32)
    for g in range(4):
        w = 2 << g
        cf[:, C_WC + g * 16:C_WC + (g + 1) * 16] = (w / np.minimum(np.arange(16) + 1, w)).astype(np.float32)[None, :]
    return cf, cfb


def fm(v, n):
    return np.ascontiguousarray(np.asarray(v, np.float32).reshape(n, 128).T)


_NC_CACHE = {}


def kernel(x, meta_tokens, lb_logits, norm_mix_g, w_in, hg_norm_g, w_pool_grp, pool_scale,
           w_br_hgrn, w_br_pool, w_out, norm_ffn_g, w_ffn_gate, w_ffn_up, w_ffn_down,
           final_norm_g, _n_tiles=NTILES, _cores=8, _trace=False):
    x = np.asarray(x, np.float32)
    meta = np.asarray(meta_tokens, np.float32)
    prm = np.zeros((128, P_TOT), np.float32)
    prm[:, P_G1:P_G1 + 8] = fm(np.asarray(norm_mix_g)[0], 8)
    prm[:, P_G2:P_G2 + 8] = fm(np.asarray(norm_ffn_g)[0], 8)
    prm[:, P_G3:P_G3 + 8] = fm(np.asarray(final_norm_g), 8)
    prm[:, P_GHG:P_GHG + 8] = fm(np.asarray(hg_norm_g)[0], 8)
    prm[:, P_PSC:P_PSC + 4] = fm(np.asarray(pool_scale)[0], 4)
    prm[:, P_L0:P_L0 + 8] = fm(np.asarray(lb_logits)[0], 8)
    prm[:, P_L1:P_L1 + 8] = fm(np.asarray(lb_logits)[1], 8)
    cf, cfb = make_consts()
    shared = {
        "w_in": np.ascontiguousarray(np.asarray(w_in, np.float32)[0]),
        "w_grp": np.ascontiguousarray(np.asarray(w_pool_grp, np.float32)[0]),
        "w_brh": np.ascontiguousarray(np.asarray(w_br_hgrn, np.float32)[0]),
        "w_brp": np.ascontiguousarray(np.asarray(w_br_pool, np.float32)[0]),
        "w_out": np.ascontiguousarray(np.asarray(w_out, np.float32)[0]),
        "w_fg": np.ascontiguousarray(np.asarray(w_ffn_gate, np.float32)[0]),
        "w_fu": np.ascontiguousarray(np.asarray(w_ffn_up, np.float32)[0]),
        "w_fd": np.ascontiguousarray(np.asarray(w_ffn_down, np.float32)[0]),
        "cf32": cf,
        "cfb32": cfb,
        "prm": prm,
    }
    in_maps = []
    for b in range(_cores):
        hT = np.ascontiguousarray(np.concatenate([meta, x[b]], axis=0).T)
        m = dict(shared)
        m["xT"] = hT
        in_maps.append(m)
    key = _n_tiles
    if key not in _NC_CACHE:
        _NC_CACHE[key] = build_nc(_n_tiles)
    nc = _NC_CACHE[key]
    res = run_bass_kernel_spmd(nc, in_maps, core_ids=list(range(_cores)), trace=_trace)
    out = np.stack([np.ascontiguousarray(r["outT"].T) for r in res.results], axis=0)
    if _trace:
        kernel.last_res = res
    return out.astype(np.float32)
```

```python
import numpy as np
import concourse.bass as bass
import concourse.mybir as mybir
from concourse.bass_utils import run_bass_kernel_spmd

F32 = mybir.dt.float32
BF16 = mybir.dt.bfloat16
AF = mybir.ActivationFunctionType
ALU = mybir.AluOpType

D = 1024
SEQ = 4096
NMETA = 16
TT = SEQ + NMETA
NTILES = 9
NTM = 460
SIZES = [457] * 8 + [456]
OFFS = [sum(SIZES[:i]) for i in range(NTILES)]
DFF = 2816
NFC = DFF // 128
EPS = 1e-6
NSLOT = 6
SLOTW = 4096
ENG = ("pe", "act", "dve", "pool", "sp")

C_MASK, C_WC = 0, 128
C_TOT = C_WC + 64
CB_IDENT, CB_ONES = 0, 128
CB_TOT = 256
P_G1, P_G2, P_G3, P_GHG, P_PSC, P_L0, P_L1 = 0, 8, 16, 24, 32, 36, 44
P_TOT = 52


def chunks_of(nt):
    base = nt // 4
    rem = nt % 4
    out = []
    c0 = 0
    for i in range(4):
        c = base + (1 if i < rem else 0)
        out.append((c0, c))
        c0 += c
    return out


class Prog:
    def __init__(self):
        self.q = {e: [] for e in ENG}
        self.cnt = {}
        self.waited = {e: {} for e in ENG}
        self.recs = {}

    def emit(self, eng, fn, r=(), w=(), sem=None, inc=1, dma=False):
        deps = {}
        rec_eng = "dma" if dma else eng

        def add(tok):
            if tok is None:
                return
            k, v = tok
            if deps.get(k, 0) < v:
                deps[k] = v

        for (key, lo, hi) in r:
            psum = isinstance(key, tuple)
            for rec in self.recs.get(key, ()):
                if (rec[3] or (psum and rec[4] != rec_eng)) and rec[0] < hi and lo < rec[1]:
                    if rec[4] == rec_eng and rec_eng == "pe":
                        continue
                    add(rec[2])
        for (key, lo, hi) in w:
            for rec in self.recs.get(key, ()):
                if rec[0] < hi and lo < rec[1]:
                    if rec[4] == rec_eng and rec_eng == "pe":
                        continue
                    add(rec[2])
        waits = []
        wd = self.waited[eng]
        for k, v in deps.items():
            if wd.get(k, 0) < v:
                wd[k] = v
                waits.append((k, v))
        semkey = sem if sem is not None else eng
        self.cnt[semkey] = self.cnt.get(semkey, 0) + inc
        tok = (semkey, self.cnt[semkey])
        self.q[eng].append((waits, fn, semkey, inc))
        for (key, lo, hi) in w:
            lst = self.recs.setdefault(key, [])
            lst[:] = [rc for rc in lst if not (lo <= rc[0] and rc[1] <= hi)]
            lst.append((lo, hi, tok, True, rec_eng))
        for (key, lo, hi) in r:
            lst = self.recs.setdefault(key, [])
            lst[:] = [rc for rc in lst if not ((not rc[3]) and rc[4] == rec_eng and lo <= rc[0] and rc[1] <= hi)]
            lst.append((lo, hi, tok, False, rec_eng))
        return tok


class Buf:
    def __init__(self, arena, key, off, n):
        self.t, self.key, self.off, self.n = arena, key, off, n

    def ap(self, a=0, b=None, p=128, p0=0):
        b = self.n if b is None else b
        assert 0 <= a < b <= self.n, (a, b, self.n)
        return self.t[p0:p, self.off + a:self.off + b]

    def rg(self, a=0, b=None):
        b = self.n if b is None else b
        return (self.key, self.off + a, self.off + b)


class Arena:
    def __init__(self, t, key, size):
        self.t, self.key, self.size, self.pos = t, key, size, 0

    def alloc(self, n, at=None):
        if at is None:
            at = self.pos
            self.pos += n + (n % 2)
        assert at + n <= self.size, (self.key, at, n, self.size)
        return Buf(self.t, self.key, at, n)


def build_nc(n_tiles=NTILES):
    nc = bass.Bass("TRN2", target_bir_lowering=False)
    dram = {}

    def din(name, shape):
        dram[name] = nc.dram_tensor(name, list(shape), F32, kind="ExternalInput").ap()
        return dram[name]

    xT = din("xT", (D, TT))
    w_in = din("w_in", (D, 6656))
    w_grp = din("w_grp", (4, 128, 128))
    w_brh = din("w_brh", (D, D))
    w_brp = din("w_brp", (512, D))
    w_out = din("w_out", (D, D))
    w_fg = din("w_fg", (D, DFF))
    w_fu = din("w_fu", (D, DFF))
    w_fd = din("w_fd", (DFF, D))
    cf32 = din("cf32", (128, C_TOT))
    cfb32 = din("cfb32", (128, CB_TOT))
    prm = din("prm", (128, P_TOT))
    outT = nc.dram_tensor("outT", [D, SEQ], F32, kind="ExternalOutput").ap()

    FSZ = 26100
    BSZ = 27700
    from contextlib import ExitStack
    with ExitStack() as es:
        fa_t = es.enter_context(nc.sbuf_tensor("fa", [128, FSZ], F32))
        ba_t = es.enter_context(nc.sbuf_tensor("ba", [128, BSZ], BF16))
        wr_t = es.enter_context(nc.sbuf_tensor("wr", [128, NSLOT * SLOTW], BF16))
        ps = [es.enter_context(nc.psum_tensor(f"ps{i}", [128, 512], F32)) for i in range(7)]
        ps7 = es.enter_context(nc.psum_tensor("ps7", [128, 1024], BF16))
        semnames = ["pe", "act", "dve", "pool", "x0", "x1", "o0", "o1", "cst0", "cst1", "cst2"] + [f"w{i}" for i in range(NSLOT)]
        sems = {n: es.enter_context(nc.semaphore("s_" + n)) for n in semnames}

        FA = Arena(fa_t, "fa", FSZ)
        BA = Arena(ba_t, "ba", BSZ)
        P = Prog()

        def PS(i):
            return (("ps", i), 0, 1)

        H = [FA.alloc(8 * NTM), FA.alloc(8 * NTM)]
        CF = FA.alloc(C_TOT)
        PRM = FA.alloc(P_TOT)
        LBV = FA.alloc(8 * 4)
        GHH = FA.alloc(8)
        ZEROS = FA.alloc(128)
        HS = []
        for i in range(3):
            HS.append(dict(qs=FA.alloc(NTM), th=FA.alloc(NTM), kk=FA.alloc(NTM), eb=FA.alloc(NTM), ln=FA.alloc(NTM), cum=FA.alloc(NTM),
                           Qb=BA.alloc(NTM), Kb=BA.alloc(NTM), KbT=BA.alloc(512)))
        THG = [FA.alloc(NTM) for _ in range(4)]
        OSBUF = [FA.alloc(NTM), FA.alloc(NTM)]
        RS = [FA.alloc(NTM), FA.alloc(NTM)]
        CFB = Buf(fa_t, "fa", HS[0]["qs"].off, CB_TOT)
        PSCW = FA.alloc(4)
        LNV = FA.alloc(NTM)
        RSTD = FA.alloc(NTM)
        RSTD1 = FA.alloc(NTM)
        DUM = FA.alloc(2)
        XPW = 16 + 458
        SCR = FA.alloc(6 * XPW)
        GT = [Buf(fa_t, "fa", SCR.off + i * NTM, NTM) for i in (0, 1)]
        T2 = [FA.alloc(NTM), FA.alloc(NTM)]
        SG = [Buf(fa_t, "fa", SCR.off + i * NTM, NTM) for i in (2, 3)]
        XP = [Buf(fa_t, "fa", SCR.off + i * XPW, XPW) for i in range(4)]
        SA = Buf(fa_t, "fa", SCR.off + 4 * XPW, XPW)
        SB = Buf(fa_t, "fa", SCR.off + 5 * XPW, XPW)
        HALO = FA.alloc(64)
        TST = FA.alloc(8 * 128)
        DPREV = FA.alloc(8)

        U = BA.alloc(8 * NTM)
        shared0 = BA.pos
        V = BA.alloc(4 * 1024)
        GO = BA.alloc(8 * NTM)
        MIX = BA.alloc(8 * NTM)
        ACT = BA.alloc(NFC * NTM, at=shared0)
        assert shared0 + NFC * NTM <= BA.pos
        ATM = [BA.alloc(128) for _ in range(4)]
        S123 = [BA.alloc(3 * 128), BA.alloc(3 * 128)]
        SQ = [BA.alloc(NTM), BA.alloc(NTM)]
        POOLED = BA.alloc(4 * NTM)
        YP = BA.alloc(4 * NTM)
        SBF = [BA.alloc(8 * 128), BA.alloc(8 * 128)]
        IDENT = BA.alloc(128)
        ONESB = BA.alloc(128)

        SLOT = [Buf(wr_t, "wr", i * SLOTW, SLOTW) for i in range(NSLOT)]

        def act(fn, r, w):
            return P.emit("act", fn, r, w)

        def dve(fn, r, w):
            return P.emit("dve", fn, r, w)

        def pool(fn, r, w):
            return P.emit("pool", fn, r, w)

        phase = dict(name="")

        def pe(fn, r, w, lab=None):
            lab = lab or phase["name"]

            def fn2(e, fn=fn, lab=lab):
                ins = fn(e)
                if lab:
                    ins.annotate(lab)
                return ins
            return P.emit("pe", fn2, r, w)

        def wspec():
            g = []
            g.append(("WI0", w_in[:, 2048:2560], 8, 512))
            g.append(("WI1", w_in[:, 2560:3072], 8, 512))
            g.append(("WXP", w_in[:, 4096:4608], 8, 512))
            g.append(("WGRP", None, 4, 128))
            g.append(("WQ0", w_in[:, 0:512], 8, 512))
            g.append(("WF0", w_in[:, 1024:1536], 8, 512))
            g.append(("WOG0", w_in[:, 3072:3584], 8, 512))
            g.append(("WQ1", w_in[:, 512:1024], 8, 512))
            g.append(("WF1", w_in[:, 1536:2048], 8, 512))
            g.append(("WOG1", w_in[:, 3584:4096], 8, 512))
            for hf in range(2):
                g.append((f"WGB{hf}", w_in[:, 5632 + hf * 512:5632 + (hf + 1) * 512], 8, 512))
                g.append((f"WBP{hf}", w_brp[:, hf * 512:(hf + 1) * 512], 4, 512))
            for hf in range(2):
                g.append((f"WGA{hf}", w_in[:, 4608 + hf * 512:4608 + (hf + 1) * 512], 8, 512))
                g.append((f"WBH{hf}", w_brh[:, hf * 512:(hf + 1) * 512], 8, 512))
            g.append(("WO0", w_out[:, 0:512], 8, 512))
            g.append(("WO1", w_out[:, 512:1024], 8, 512))
            for i in range(6):
                nc_ = 512 if i < 5 else 256
                g.append((f"WG{i}", w_fg[:, i * 512:i * 512 + nc_], 8, nc_))
                g.append((f"WU{i}", w_fu[:, i * 512:i * 512 + nc_], 8, nc_))
            for j in range(8):
                g.append((f"WD{j}", w_fd[:, j * 128:(j + 1) * 128], NFC, 128))
            return g

        WS = wspec()
        NG = len(WS)
        GIDX = {sp[0]: i for i, sp in enumerate(WS)}
        wst = dict(next_load=0, free=list(range(NSLOT)), loaded={})

        def issue_load(gi, slot):
            ti, g = divmod(gi, NG)
            name, src, nkc, ncols = WS[g]
            sb = SLOT[slot]
            n = nkc * ncols
            assert n <= SLOTW
            if name == "WGRP":
                src_ap = w_grp.rearrange("g c d -> c g d")
            else:
                src_ap = src.rearrange("(kc p) c -> p kc c", p=128)
            dst_ap = sb.ap(0, n).rearrange("p (kc c) -> p kc c", kc=nkc)

            def fn(e, dst_ap=dst_ap, src_ap=src_ap):
                return e.dma_start(out=dst_ap, in_=src_ap)
            P.emit("pool", fn, r=[], w=[sb.rg(0, n)], sem=f"w{slot}", inc=16, dma=True)

        def pump():
            while wst["next_load"] < n_tiles * NG and wst["free"]:
                slot = wst["free"].pop(0)
                issue_load(wst["next_load"], slot)
                wst["loaded"][wst["next_load"]] = slot
                wst["next_load"] += 1

        def need(ti, name):
            g = GIDX[name]
            gi = ti * NG + g
            pump()
            assert gi in wst["loaded"], ("weight group not resident (ring too small / order)", ti, name)
            return SLOT[wst["loaded"][gi]], WS[g][3]

        def rel(ti, name):
            gi = ti * NG + GIDX[name]
            wst["free"].append(wst["loaded"].pop(gi))
            pump()

        def prologue():
            P.emit("sp", lambda e: e.dma_start(out=CF.ap(), in_=cf32[:, :]), r=[], w=[CF.rg()], sem="cst0", inc=16, dma=True)
            P.emit("sp", lambda e: e.dma_start(out=PRM.ap(), in_=prm[:, :]), r=[], w=[PRM.rg()], sem="cst1", inc=16, dma=True)
            P.emit("sp", lambda e: e.dma_start(out=CFB.ap(), in_=cfb32[:, :]), r=[], w=[CFB.rg()], sem="cst2", inc=16, dma=True)
            dve(lambda e: e.tensor_copy(out=IDENT.ap(), in_=CFB.ap(CB_IDENT, CB_IDENT + 128)), [CFB.rg()], [IDENT.rg()])
            dve(lambda e: e.tensor_copy(out=ONESB.ap(), in_=CFB.ap(CB_ONES, CB_ONES + 128)), [CFB.rg()], [ONESB.rg()])
            dve(lambda e: e.memset(ZEROS.ap(), 0.0), [], [ZEROS.rg()])
            dve(lambda e: e.memset(TST.ap(), 0.0), [], [TST.rg()])
            dve(lambda e: e.memset(SBF[0].ap(), 0.0), [], [SBF[0].rg()])
            dve(lambda e: e.memset(DPREV.ap(), 0.0), [], [DPREV.rg()])
            for g in range(4):
                dve(lambda e, g=g: e.tensor_scalar(out=PSCW.ap(g, g + 1), in0=PRM.ap(P_PSC + g, P_PSC + g + 1), scalar1=1.0 / (2 << g),
                                                   scalar2=None, op0=ALU.mult),
                    [PRM.rg()], [PSCW.rg(g, g + 1)])
            dve(lambda e: e.memset(HALO.ap(), 0.0), [], [HALO.rg()])
            dve(lambda e: e.tensor_tensor(out=LBV.ap(24, 32), in0=PRM.ap(P_L0, P_L0 + 8), in1=PRM.ap(P_L1, P_L1 + 8), op=ALU.subtract),
                [PRM.rg()], [LBV.rg(24, 32)])
            act(lambda e: e.activation(out=LBV.ap(24, 32), in_=LBV.ap(24, 32), func=AF.Tanh, scale=0.5), [LBV.rg(24, 32)], [LBV.rg(24, 32)])
            dve(lambda e: e.tensor_scalar(out=LBV.ap(0, 8), in0=LBV.ap(24, 32), scalar1=0.25, scalar2=0.75, op0=ALU.mult, op1=ALU.add),
                [LBV.rg(24, 32)], [LBV.rg(0, 8)])
            dve(lambda e: e.tensor_scalar(out=LBV.ap(8, 16), in0=LBV.ap(24, 32), scalar1=-0.25, scalar2=0.25, op0=ALU.mult, op1=ALU.add),
                [LBV.rg(24, 32)], [LBV.rg(8, 16)])
            dve(lambda e: e.tensor_scalar(out=LBV.ap(16, 24), in0=LBV.ap(24, 32), scalar1=0.25, scalar2=-0.25, op0=ALU.mult, op1=ALU.add),
                [LBV.rg(24, 32)], [LBV.rg(16, 24)])
            dve(lambda e: e.tensor_scalar(out=GHH.ap(), in0=PRM.ap(P_GHG, P_GHG + 8), scalar1=0.5, scalar2=None, op0=ALU.mult),
                [PRM.rg()], [GHH.rg()])

        def load_x(ti):
            nt, t0 = SIZES[ti], OFFS[ti]
            hb = H[ti % 2]
            dst = hb.ap().rearrange("p (kc n) -> p kc n", kc=8)[:, :, 0:nt]
            src = xT[:, t0:t0 + nt].rearrange("(kc p) n -> p kc n", p=128)
            P.emit("sp", lambda e: e.dma_start(out=dst, in_=src), r=[], w=[hb.rg()], sem=f"x{ti % 2}", inc=16, dma=True)

        def store_out(ti):
            nt, t0 = SIZES[ti], OFFS[ti]
            hb = H[ti % 2]
            n0 = NMETA if ti == 0 else 0
            src = hb.ap().rearrange("p (kc n) -> p kc n", kc=8)[:, :, n0:nt]
            dst = outT[:, t0 + n0 - NMETA:t0 + nt - NMETA].rearrange("(kc p) n -> p kc n", p=128)
            P.emit("sp", lambda e: e.dma_start(out=dst, in_=src), r=[hb.rg()], w=[], sem=f"o{ti % 2}", inc=16, dma=True)

        def norm_sq(src, nt, kc):
            sq = SQ[kc % 2]
            act(lambda e, sq=sq, kc=kc: e.activation(out=sq.ap(0, nt), in_=src.ap(kc * NTM, kc * NTM + nt), func=AF.Square),
                [src.rg(kc * NTM, kc * NTM + nt)], [sq.rg(0, nt)])
            pe(lambda e, sq=sq, kc=kc: e.matmul(ps[6][:, 0:nt], lhsT=ONESB.ap(), rhs=sq.ap(0, nt), start=(kc == 0), stop=(kc == 7)),
               [ONESB.rg(), sq.rg(0, nt)], [PS(6)])

        def norm_fin(nt, rstd):
            act(lambda e: e.activation(out=LNV.ap(0, nt), in_=ps[6][:, 0:nt], func=AF.Ln, scale=1.0 / D, bias=EPS),
                [PS(6)], [LNV.rg(0, nt)])
            act(lambda e: e.activation(out=rstd.ap(0, nt), in_=LNV.ap(0, nt), func=AF.Exp, scale=-0.5),
                [LNV.rg(0, nt)], [rstd.rg(0, nt)])

        def norm_apply(src, nt, gcol, dst, rstd):
            for kc in range(8):
                dve(lambda e, kc=kc: e.scalar_tensor_tensor(out=dst.ap(kc * NTM, kc * NTM + nt), in0=src.ap(kc * NTM, kc * NTM + nt),
                                                            scalar=PRM.ap(gcol + kc, gcol + kc + 1), in1=rstd.ap(0, nt),
                                                            op0=ALU.mult, op1=ALU.mult),
                    [src.rg(kc * NTM, kc * NTM + nt), PRM.rg(), rstd.rg(0, nt)], [dst.rg(kc * NTM, kc * NTM + nt)])

        def rmsnorm(src, nt, gcol, dst, rstd=None):
            rstd = rstd or RSTD
            for kc in range(8):
                norm_sq(src, nt, kc)
            norm_fin(nt, rstd)
            norm_apply(src, nt, gcol, dst, rstd)

        def table_preswitch_ln():
            act(lambda e: e.activation(out=DUM.ap(0, 1), in_=ZEROS.ap(0, 1), func=AF.Ln, bias=1.0), [ZEROS.rg(0, 1)], [DUM.rg(0, 1)])

        def proj(bank, wslot, ncols, col, xbuf, nt, nkc=8, xstride=NTM, lab=None, split=False):
            if split:
                for kc in range(nkc):
                    pe(lambda e, kc=kc: e.matmul(ps[bank][:, 0:nt], lhsT=wslot.ap(kc * ncols + col, kc * ncols + col + 128),
                                                 rhs=xbuf.ap(kc * xstride, kc * xstride + nt), start=(kc == 0), stop=(kc == nkc - 1)),
                       [wslot.rg(0, nkc * ncols), xbuf.rg(kc * xstride, kc * xstride + nt)], [PS(bank)], lab=lab)
                return

            def fn(e):
                last = None
                for kc in range(nkc):
                    last = e.matmul(ps[bank][:, 0:nt], lhsT=wslot.ap(kc * ncols + col, kc * ncols + col + 128),
                                    rhs=xbuf.ap(kc * xstride, kc * xstride + nt), start=(kc == 0), stop=(kc == nkc - 1))
                return last
            pe(fn, [wslot.rg(0, nkc * ncols), xbuf.rg(0, (nkc - 1) * xstride + nt)], [PS(bank)], lab=lab)

        R6 = [0, 1, 2, 3, 4, 5]
        R3 = [0, 1, 2]
        ring = dict(i=0)

        held = set()

        def rbank(banks):
            for _ in range(len(banks)):
                b = banks[ring["i"] % len(banks)]
                ring["i"] += 1
                if b not in held:
                    held.add(b)
                    return b
            raise AssertionError(("no free PSUM bank", banks, sorted(held)))

        def rfree(*bs):
            for b in bs:
                held.discard(b)

        def gbank(banks):
            while all(b in held for b in banks):
                yield "blocked"
            return rbank(banks)

        def tile_prog(ti):
            nt, t0 = SIZES[ti], OFFS[ti]
            hb = H[ti % 2]
            chunks = chunks_of(nt)
            cmax = max(c for _, c in chunks)
            if ti + 1 < n_tiles:
                load_x(ti + 1)
            phase['name'] = f't{ti}.V'
            flip = 0
            for hf in range(2):
                wsl, _ = need(ti, f"WI{hf}")
                for ci, (c0, c) in enumerate(chunks):
                    b = rbank([0, 1, 2, 3, 4, 5])

                    def fn(e, b=b, c0=c0, c=c, wsl=wsl):
                        last = None
                        for kc in range(8):
                            last = e.matmul(ps[b][0:c, 0:512], lhsT=U.ap(kc * NTM + c0, kc * NTM + c0 + c),
                                            rhs=wsl.ap(kc * 512, (kc + 1) * 512), start=(kc == 0), stop=(kc == 7))
                        return last
                    pe(fn, [U.rg(), wsl.rg()], [PS(b)])
                    vo = ci * 1024 + hf * 512
                    if flip % 2 == 0:
                        act(lambda e, b=b, c=c, vo=vo: e.activation(out=V.ap(vo, vo + 512, p=c), in_=ps[b][0:c, 0:512], func=AF.Copy),
                            [PS(b)], [V.rg(vo, vo + 512)])
                    else:
                        dve(lambda e, b=b, c=c, vo=vo: e.tensor_copy(out=V.ap(vo, vo + 512, p=c), in_=ps[b][0:c, 0:512]),
                            [PS(b)], [V.rg(vo, vo + 512)])
                    rfree(b)
                    flip += 1
                rel(ti, f"WI{hf}")

            def pool_mixer_1():
                lab = f't{ti}.poolmix'
                wxp, _ = need(ti, "WXP")
                for g in range(4):
                    w = 2 << g
                    b = rbank(R6)
                    proj(b, wxp, 512, g * 128, U, nt, lab=lab)
                    xp = XP[g]
                    act(lambda e, b=b, xp=xp: e.activation(out=xp.ap(16, 16 + nt), in_=ps[b][:, 0:nt], func=AF.Copy),
                        [PS(b)], [xp.rg(16, 16 + nt)])
                    rfree(b)
                    pool(lambda e, xp=xp, g=g: e.tensor_copy(out=xp.ap(0, 16), in_=HALO.ap(g * 16, (g + 1) * 16)),
                         [HALO.rg(g * 16, (g + 1) * 16)], [xp.rg(0, 16)])
                    end = 16 + nt
                    srcb = xp
                    bufs = [SA, SB]
                    k = 0
                    sh = 1
                    lo = 1
                    while sh < w:
                        dstb = bufs[k % 2]
                        pool(lambda e, srcb=srcb, dstb=dstb, lo=lo, sh=sh, end=end: e.tensor_tensor(
                            out=dstb.ap(lo, end), in0=srcb.ap(lo, end), in1=srcb.ap(lo - sh, end - sh), op=ALU.add),
                            [srcb.rg(lo - sh, end)], [dstb.rg(lo, end)])
                        srcb = dstb
                        k += 1
                        sh *= 2
                        lo += sh
                    sw = srcb
                    dve(lambda e, sw=sw, xp=xp, g=g, w=w, end=end: e.scalar_tensor_tensor(
                        out=POOLED.ap(g * NTM, g * NTM + nt), in0=xp.ap(16, end), scalar=-float(w), in1=sw.ap(16, end),
                        op0=ALU.mult, op1=ALU.add),
                        [xp.rg(16, end), sw.rg(16, end)], [POOLED.rg(g * NTM, g * NTM + nt)])
                    if ti == 0:
                        tmp = RS[0]
                        dve(lambda e, sw=sw, tmp=tmp, g=g: e.tensor_tensor(out=tmp.ap(0, 16), in0=sw.ap(16, 32),
                                                                          in1=CF.ap(C_WC + g * 16, C_WC + (g + 1) * 16), op=ALU.mult),
                            [sw.rg(16, 32), CF.rg()], [tmp.rg(0, 16)])
                        dve(lambda e, tmp=tmp, xp=xp, g=g, w=w: e.scalar_tensor_tensor(
                            out=POOLED.ap(g * NTM, g * NTM + 16), in0=xp.ap(16, 32), scalar=-float(w), in1=tmp.ap(0, 16),
                            op0=ALU.mult, op1=ALU.add),
                            [xp.rg(16, 32), tmp.rg(0, 16)], [POOLED.rg(g * NTM, g * NTM + 16)])
                    pool(lambda e, xp=xp, g=g: e.tensor_copy(out=HALO.ap(g * 16, (g + 1) * 16), in_=xp.ap(nt, nt + 16)),
                         [xp.rg(nt, nt + 16)], [HALO.rg(g * 16, (g + 1) * 16)])
                rel(ti, "WXP")

            def pool_mixer_2():
                lab = f't{ti}.poolmix2'
                wgrp, _ = need(ti, "WGRP")
                for g in range(4):
                    b2 = rbank(R3)
                    pe(lambda e, b2=b2, g=g: e.matmul(ps[b2][:, 0:nt], lhsT=wgrp.ap(g * 128, (g + 1) * 128),
                                                      rhs=POOLED.ap(g * NTM, g * NTM + nt), start=True, stop=True),
                       [wgrp.rg(0, 512), POOLED.rg(g * NTM, g * NTM + nt)], [PS(b2)], lab=lab)
                    act(lambda e, b2=b2, g=g: e.activation(out=YP.ap(g * NTM, g * NTM + nt), in_=ps[b2][:, 0:nt], func=AF.Identity,
                                                           scale=PSCW.ap(g, g + 1)),
                        [PS(b2), PSCW.rg()], [YP.rg(g * NTM, g * NTM + nt)])
                    rfree(b2)
                rel(ti, "WGRP")

            done = set()

            def wait_for(kind, h):
                while h >= 0 and (kind, h) not in done:
                    yield "blocked"

            def gen_a(h):
                lab = f't{ti}.A{h}'
                yield from wait_for("A", h - 3)
                yield from wait_for("C", h - 4)
                hf, hc = divmod(h, 4)
                hc *= 128
                wq, _ = need(ti, f"WQ{hf}")
                wf, _ = need(ti, f"WF{hf}")
                wog, _ = need(ti, f"WOG{hf}")
                s = HS[h % 3]
                thg = THG[h % 4]
                qb = yield from gbank(R3)
                proj(qb, wq, 512, hc, U, nt, lab=lab)
                yield
                fb = yield from gbank(R3)
                proj(fb, wf, 512, hc, U, nt, lab=lab)
                yield
                gb = yield from gbank(R3)
                proj(gb, wog, 512, hc, U, nt, lab=lab)
                yield
                act(lambda e: e.activation(out=s["qs"].ap(0, nt), in_=ps[qb][:, 0:nt], func=AF.Silu), [PS(qb)], [s["qs"].rg(0, nt)])
                act(lambda e: e.activation(out=s["th"].ap(0, nt), in_=ps[fb][:, 0:nt], func=AF.Tanh, scale=0.5), [PS(fb)], [s["th"].rg(0, nt)])
                act(lambda e: e.activation(out=thg.ap(0, nt), in_=ps[gb][:, 0:nt], func=AF.Tanh, scale=0.5), [PS(gb)], [thg.rg(0, nt)])
                rfree(qb, fb, gb)
                yield "mid"
                act(lambda e: e.activation(out=s["ln"].ap(0, nt), in_=s["th"].ap(0, nt), func=AF.Ln, scale=LBV.ap(8 + h, 9 + h),
                                           bias=LBV.ap(h, h + 1)),
                    [s["th"].rg(0, nt), LBV.rg()], [s["ln"].rg(0, nt)])
                dve(lambda e: e.tensor_scalar(out=s["kk"].ap(0, nt), in0=s["th"].ap(0, nt), scalar1=LBV.ap(16 + h, 17 + h),
                                              scalar2=LBV.ap(8 + h, 9 + h), op0=ALU.mult, op1=ALU.add),
                    [s["th"].rg(0, nt), LBV.rg()], [s["kk"].rg(0, nt)])
                yield
                yield from wait_for("B", h - 3)
                for (c0, c) in chunks:
                    dve(lambda e, c0=c0, c=c: e.tensor_tensor_scan(out=s["cum"].ap(c0, c0 + c), data0=s["ln"].ap(c0, c0 + c),
                                                                   data1=ZEROS.ap(0, c), initial=0.0, op0=ALU.add, op1=ALU.add),
                        [s["ln"].rg(c0, c0 + c), ZEROS.rg()], [s["cum"].rg(c0, c0 + c)])
                    yield
                act(lambda e: e.activation(out=s["eb"].ap(0, nt), in_=s["cum"].ap(0, nt), func=AF.Exp), [s["cum"].rg(0, nt)], [s["eb"].rg(0, nt)])
                act(lambda e: e.activation(out=s["th"].ap(0, nt), in_=s["cum"].ap(0, nt), func=AF.Exp, scale=-1.0),
                    [s["cum"].rg(0, nt)], [s["th"].rg(0, nt)])
                yield
                pool(lambda e: e.tensor_tensor(out=s["Qb"].ap(0, nt), in0=s["qs"].ap(0, nt), in1=s["eb"].ap(0, nt), op=ALU.mult),
                     [s["qs"].rg(0, nt), s["eb"].rg(0, nt)], [s["Qb"].rg(0, nt)])
                dve(lambda e: e.tensor_tensor(out=s["Kb"].ap(0, nt), in0=s["kk"].ap(0, nt), in1=s["th"].ap(0, nt), op=ALU.mult),
                    [s["kk"].rg(0, nt), s["th"].rg(0, nt)], [s["Kb"].rg(0, nt)])
                yield

                def tr(e):
                    last = None
                    for ci, (c0, c) in enumerate(chunks):
                        last = e.transpose(out=ps7[0:c, ci * 128:(ci + 1) * 128], in_=s["Kb"].ap(c0, c0 + c), identity=IDENT.ap())
                    return last
                pe(tr, [s["Kb"].rg(0, nt), IDENT.rg()], [PS(7)], lab=lab)
                ci = 0
                while ci < 4:
                    cj = ci
                    while cj + 1 < 4 and chunks[cj + 1][1] == chunks[ci][1]:
                        cj += 1
                    c = chunks[ci][1]
                    dve(lambda e, ci=ci, cj=cj, c=c: e.tensor_copy(out=s["KbT"].ap(ci * 128, (cj + 1) * 128, p=c),
                                                                  in_=ps7[0:c, ci * 128:(cj + 1) * 128]),
                        [PS(7)], [s["KbT"].rg(ci * 128, (cj + 1) * 128)])
                    ci = cj + 1
                if h % 4 == 3:
                    rel(ti, f"WQ{hf}")
                    rel(ti, f"WF{hf}")
                    rel(ti, f"WOG{hf}")
                yield

            def gen_b(h):
                lab = f't{ti}.B{h}'
                yield from wait_for("B", h - 1)
                s = HS[h % 3]
                hcol = h * 128
                st = S123[h % 2]
                s_in = SBF[ti % 2]
                s_out = SBF[(ti + 1) % 2]
                S_ap = [s_in.ap(hcol, hcol + 128)] + [st.ap(i * 128, (i + 1) * 128) for i in range(3)]
                S_rg = [s_in.rg(hcol, hcol + 128)] + [st.rg(i * 128, (i + 1) * 128) for i in range(3)]
                Sn_ap = [st.ap(i * 128, (i + 1) * 128) for i in range(3)] + [s_out.ap(hcol, hcol + 128)]
                Sn_rg = [st.rg(i * 128, (i + 1) * 128) for i in range(3)] + [s_out.rg(hcol, hcol + 128)]
                Tb = (TST.ap(hcol, hcol + 128), TST.rg(hcol, hcol + 128))

                def at_all(e):
                    last = None
                    for ci, (c0, c) in enumerate(chunks):
                        last = e.matmul(ps[4][0:c, ci * 128:ci * 128 + c], lhsT=s["Kb"].ap(c0, c0 + c), rhs=s["Qb"].ap(c0, c0 + c),
                                        start=True, stop=True)
                    return last
                pe(at_all, [s["Kb"].rg(0, nt), s["Qb"].rg(0, nt)], [PS(4)], lab=lab)

                def p_all(e):
                    last = None
                    for ci, (c0, c) in enumerate(chunks):
                        vo = ci * 1024 + hcol
                        last = e.matmul(ps[5][:, ci * 128:(ci + 1) * 128], lhsT=s["KbT"].ap(ci * 128, (ci + 1) * 128, p=c),
                                        rhs=V.ap(vo, vo + 128, p=c), start=True, stop=True)
                    return last
                pe(p_all, [s["KbT"].rg(), V.rg()], [PS(5)], lab=lab)
                yield
                for ci, (c0, c) in enumerate(chunks):
                    am = ATM[ci]
                    dve(lambda e, c=c, am=am, ci=ci: e.tensor_tensor(out=am.ap(0, c, p=c), in0=ps[4][0:c, ci * 128:ci * 128 + c],
                                                                    in1=CF.ap(C_MASK, C_MASK + c, p=c), op=ALU.mult),
                        [PS(4), CF.rg()], [am.rg()])
                for ci, (c0, c) in enumerate(chunks):
                    if ci == 0:
                        dec_ap, dec_rg = DPREV.ap(h, h + 1), DPREV.rg(h, h + 1)
                    else:
                        pc0, pc = chunks[ci - 1]
                        dec_ap, dec_rg = s["eb"].ap(pc0 + pc - 1, pc0 + pc), s["eb"].rg(pc0 + pc - 1, pc0 + pc)
                    dve(lambda e, dec_ap=dec_ap, ci=ci: e.scalar_tensor_tensor(out=Tb[0], in0=Tb[0], scalar=dec_ap, in1=ps[5][:, ci * 128:(ci + 1) * 128],
                                                                               op0=ALU.mult, op1=ALU.add),
                        [Tb[1], dec_rg, PS(5)], [Tb[1]])
                    dve(lambda e, c0=c0, c=c, ci=ci: e.tensor_scalar(out=Sn_ap[ci], in0=Tb[0], scalar1=s["eb"].ap(c0 + c - 1, c0 + c), scalar2=None,
                                                                     op0=ALU.mult),
                        [Tb[1], s["eb"].rg(c0 + c - 1, c0 + c)], [Sn_rg[ci]])
                yield

                def o_all(e):
                    last = None
                    for ci, (c0, c) in enumerate(chunks):
                        vo = ci * 1024 + hcol
                        e.matmul(ps[3][:, c0:c0 + c], lhsT=V.ap(vo, vo + 128, p=c), rhs=ATM[ci].ap(0, c, p=c), start=True, stop=False)
                        last = e.matmul(ps[3][:, c0:c0 + c], lhsT=S_ap[ci], rhs=s["Qb"].ap(c0, c0 + c), start=False, stop=True)
                    return last
                pe(o_all, [V.rg(), s["Qb"].rg(0, nt)] + [a_.rg() for a_ in ATM] + S_rg, [PS(3)], lab=lab)
                yield
                osb = OSBUF[h % 2]
                sq = SQ[h % 2]
                yield from wait_for("C", h - 2)
                dve(lambda e: e.tensor_copy(out=osb.ap(0, nt), in_=ps[3][:, 0:nt]), [PS(3)], [osb.rg(0, nt)])
                act(lambda e: e.activation(out=sq.ap(0, nt), in_=ps[3][:, 0:nt], func=AF.Square), [PS(3)], [sq.rg(0, nt)])
                dve(lambda e: e.tensor_copy(out=DPREV.ap(h, h + 1), in_=s["eb"].ap(nt - 1, nt)), [s["eb"].rg(nt - 1, nt)], [DPREV.rg(h, h + 1)])
                yield

            def gen_c(h):
                lab = f't{ti}.C{h}'
                yield from wait_for("C", h - 1)
                osb = OSBUF[h % 2]
                sq = SQ[h % 2]
                rs = RS[h % 2]
                t2 = T2[h % 2]
                thg = THG[h % 4]
                pe(lambda e: e.matmul(ps[6][:, 0:nt], lhsT=ONESB.ap(), rhs=sq.ap(0, nt), start=True, stop=True),
                   [ONESB.rg(), sq.rg(0, nt)], [PS(6)], lab=lab)
                yield
                act(lambda e: e.activation(out=rs.ap(0, nt), in_=ps[6][:, 0:nt], func=AF.Ln, scale=1.0 / 128, bias=EPS), [PS(6)], [rs.rg(0, nt)])
                act(lambda e: e.activation(out=rs.ap(0, nt), in_=rs.ap(0, nt), func=AF.Exp, scale=-0.5), [rs.rg(0, nt)], [rs.rg(0, nt)])
                yield
                dve(lambda e: e.scalar_tensor_tensor(out=t2.ap(0, nt), in0=osb.ap(0, nt), scalar=GHH.ap(h, h + 1), in1=rs.ap(0, nt),
                                                     op0=ALU.mult, op1=ALU.mult),
                    [osb.rg(0, nt), GHH.rg(), rs.rg(0, nt)], [t2.rg(0, nt)])
                yield
                dve(lambda e: e.scalar_tensor_tensor(out=GO.ap(h * NTM, h * NTM + nt), in0=thg.ap(0, nt), scalar=1.0, in1=t2.ap(0, nt),
                                                     op0=ALU.add, op1=ALU.mult),
                    [thg.rg(0, nt), t2.rg(0, nt)], [GO.rg(h * NTM, h * NTM + nt)])
                yield

            def gen_m(j0, j1):
                lab = f't{ti}.M'
                for j in range(j0, j1):
                    hf, jc = divmod(j, 4)
                    jc *= 128
                    wgb, _ = need(ti, f"WGB{hf}")
                    wbp, _ = need(ti, f"WBP{hf}")
                    b = yield from gbank(R3)
                    proj(b, wgb, 512, jc, U, nt, lab=lab)
                    yield
                    gb2 = GT[1]
                    act(lambda e, b=b, gb2=gb2: e.activation(out=gb2.ap(0, nt), in_=ps[b][:, 0:nt], func=AF.Tanh, scale=0.5), [PS(b)], [gb2.rg(0, nt)])
                    rfree(b)
                    b2 = yield from gbank(R3)
                    proj(b2, wbp, 512, jc, YP, nt, nkc=4, lab=lab)
                    yield
                    dve(lambda e, gb2=gb2, b2=b2, j=j: e.scalar_tensor_tensor(out=MIX.ap(j * NTM, j * NTM + nt), in0=gb2.ap(0, nt), scalar=1.0,
                                                                            in1=ps[b2][:, 0:nt], op0=ALU.add, op1=ALU.mult),
                        [gb2.rg(0, nt), PS(b2)], [MIX.rg(j * NTM, j * NTM + nt)])
                    rfree(b2)
                    if j % 4 == 3:
                        rel(ti, f"WGB{hf}")
                        rel(ti, f"WBP{hf}")
                    yield

            pool_mixer_1()
            active = []

            def start(kind, h):
                if kind == "M":
                    g_ = gen_m(0, 4)
                else:
                    g_ = {"A": gen_a, "B": gen_b, "C": gen_c}[kind](h)
                active.append((kind, h, g_))

            start("A", 0)
            PRIO = {"B": 0, "C": 1, "A": 2, "M": 3}
            while active:
                progressed = False
                for item in sorted(active, key=lambda it: (PRIO[it[0]], it[1])):
                    kind, h, g_ = item
                    try:
                        sig = next(g_)
                        if sig != "blocked":
                            progressed = True
                    except StopIteration:
                        progressed = True
                        active.remove(item)
                        done.add((kind, h))
                        if kind == "A":
                            start("B", h)
                            if h == 1:
                                pool_mixer_2()
                            if h == 5:
                                start("M", 0)
                        elif kind == "B":
                            start("C", h)
                        continue
                    if sig == "mid" and kind == "A" and h + 1 < 8:
                        start("A", h + 1)
                        progressed = True
                if not progressed:
                    raise AssertionError(("HGRN scheduler stuck", [(k_, h_) for k_, h_, _ in active]))

            phase['name'] = f't{ti}.merge'
            for _ in gen_m(4, 8):
                pass
            for j in range(8):
                hf, jc = divmod(j, 4)
                jc *= 128
                wga, _ = need(ti, f"WGA{hf}")
                wbh, _ = need(ti, f"WBH{hf}")
                bga = rbank(R6)
                proj(bga, wga, 512, jc, U, nt)
                bya = rbank(R6)
                proj(bya, wbh, 512, jc, GO, nt)
                ga = GT[0]
                m1 = T2[j % 2]
                act(lambda e, bga=bga, ga=ga: e.activation(out=ga.ap(0, nt), in_=ps[bga][:, 0:nt], func=AF.Tanh, scale=0.5), [PS(bga)], [ga.rg(0, nt)])
                dve(lambda e, ga=ga, m1=m1, bya=bya: e.scalar_tensor_tensor(out=m1.ap(0, nt), in0=ga.ap(0, nt), scalar=1.0, in1=ps[bya][:, 0:nt],
                                                                            op0=ALU.add, op1=ALU.mult),
                    [ga.rg(0, nt), PS(bya)], [m1.rg(0, nt)])
                rfree(bga, bya)
                pool(lambda e, m1=m1, j=j: e.tensor_tensor(out=MIX.ap(j * NTM, j * NTM + nt), in0=m1.ap(0, nt), in1=MIX.ap(j * NTM, j * NTM + nt), op=ALU.add),
                     [m1.rg(0, nt), MIX.rg(j * NTM, j * NTM + nt)], [MIX.rg(j * NTM, j * NTM + nt)])
                if j % 4 == 3:
                    rel(ti, f"WGA{hf}")
                    rel(ti, f"WBH{hf}")

            table_preswitch_ln()
            phase['name'] = f't{ti}.oproj'
            for j in range(8):
                hf, jc = divmod(j, 4)
                jc *= 128
                wo, _ = need(ti, f"WO{hf}")
                b = rbank([0, 1, 2, 3, 4, 5])
                proj(b, wo, 512, jc, MIX, nt)
                dve(lambda e, b=b, j=j: e.scalar_tensor_tensor(out=hb.ap(j * NTM, j * NTM + nt), in0=ps[b][:, 0:nt], scalar=0.5,
                                                               in1=hb.ap(j * NTM, j * NTM + nt), op0=ALU.mult, op1=ALU.add),
                    [PS(b), hb.rg(j * NTM, j * NTM + nt)], [hb.rg(j * NTM, j * NTM + nt)])
                rfree(b)
                if j % 4 == 3:
                    rel(ti, f"WO{hf}")
                if j >= 1:
                    norm_sq(hb, nt, j - 1)
            norm_sq(hb, nt, 7)

            phase['name'] = f't{ti}.ffn'
            norm_fin(nt, RSTD)
            norm_apply(hb, nt, P_G2, U, RSTD)
            for fc in range(NFC):
                g, col = divmod(fc, 4)
                col *= 128
                wg, ncg = need(ti, f"WG{g}")
                wu, ncu = need(ti, f"WU{g}")
                bg = rbank([0, 1, 2, 3, 4, 5])
                proj(bg, wg, ncg, col, U, nt, split=(fc == 0))
                bu = rbank([0, 1, 2, 3, 4, 5])
                proj(bu, wu, ncu, col, U, nt, split=(fc == 0))
                sg = SG[fc % 2]
                act(lambda e, bg=bg, sg=sg: e.activation(out=sg.ap(0, nt), in_=ps[bg][:, 0:nt], func=AF.Silu), [PS(bg)], [sg.rg(0, nt)])
                dve(lambda e, bu=bu, sg=sg, fc=fc: e.tensor_tensor(out=ACT.ap(fc * NTM, fc * NTM + nt), in0=ps[bu][:, 0:nt], in1=sg.ap(0, nt), op=ALU.mult),
                    [PS(bu), sg.rg(0, nt)], [ACT.rg(fc * NTM, fc * NTM + nt)])
                rfree(bg, bu)
                if fc % 4 == 3 or fc == NFC - 1:
                    rel(ti, f"WG{g}")
                    rel(ti, f"WU{g}")
            if ti + 1 < n_tiles:
                phase['name'] = f't{ti + 1}.norm1'
                rmsnorm(H[(ti + 1) % 2], SIZES[ti + 1], P_G1, U, rstd=RSTD1)
            phase['name'] = f't{ti}.down'
            for j in range(8):
                wd, _ = need(ti, f"WD{j}")
                b = rbank([0, 1, 2, 3, 4, 5])
                proj(b, wd, 128, 0, ACT, nt, nkc=NFC)
                rel(ti, f"WD{j}")
                dve(lambda e, b=b, j=j: e.tensor_tensor(out=hb.ap(j * NTM, j * NTM + nt), in0=ps[b][:, 0:nt], in1=hb.ap(j * NTM, j * NTM + nt), op=ALU.add),
                    [PS(b), hb.rg(j * NTM, j * NTM + nt)], [hb.rg(j * NTM, j * NTM + nt)])
                rfree(b)
            phase['name'] = f't{ti}.fin'
            rmsnorm(hb, nt, P_G3, hb)
            store_out(ti)

        prologue()
        load_x(0)
        phase['name'] = 't0.norm1'
        rmsnorm(H[0], SIZES[0], P_G1, U, rstd=RSTD1)
        for ti in range(n_tiles):
            tile_prog(ti)
        fin = []
        for k in ("o0", "o1"):
            if P.cnt.get(k, 0) > 0:
                fin.append((k, P.cnt[k]))

        def replay(engname, e):
            semE = sems
            for (waits, fn, semkey, inc) in P.q[engname]:
                for (k, v) in waits:
                    e.wait_ge(semE[k], v)
                ins = fn(e)
                ins.then_inc(semE[semkey], inc)

        with nc.Block() as block:
            @block.tensor
            def _(e):
                replay("pe", e)

            @block.scalar
            def _(e):
                replay("act", e)

            @block.vector
            def _(e):
                replay("dve", e)

            @block.gpsimd
            def _(e):
                replay("pool", e)

            @block.sync
            def _(e):
                replay("sp", e)
                for (k, v) in fin:
                    e.wait_ge(sems[k], v)
    return nc


def make_consts():
    cf = np.zeros((128, C_TOT), np.float32)
    cfb = np.zeros((128, CB_TOT), np.float32)
    cfb[:, CB_IDENT:CB_IDENT + 128] = np.eye(128, dtype=np.float32)
    cfb[:, CB_ONES:CB_ONES + 128] = 1.0
    s = np.arange(128)[:, None]
    t = np.arange(128)[None, :]
    cf[:, C_MASK:C_MASK + 128] = (s <= t).astype(np.float32)
    for g in range(4):
        w = 2 << g
        cf[:, C_WC + g * 16:C_WC + (g + 1) * 16] = (w / np.minimum(np.arange(16) + 1, w)).astype(np.float32)[None, :]
    return cf, cfb


def fm(v, n):
    return np.ascontiguousarray(np.asarray(v, np.float32).reshape(n, 128).T)


_NC_CACHE = {}


def kernel(x, meta_tokens, lb_logits, norm_mix_g, w_in, hg_norm_g, w_pool_grp, pool_scale,
           w_br_hgrn, w_br_pool, w_out, norm_ffn_g, w_ffn_gate, w_ffn_up, w_ffn_down,
           final_norm_g, _n_tiles=NTILES, _cores=8, _trace=False):
    x = np.asarray(x, np.float32)
    meta = np.asarray(meta_tokens, np.float32)
    prm = np.zeros((128, P_TOT), np.float32)
    prm[:, P_G1:P_G1 + 8] = fm(np.asarray(norm_mix_g)[0], 8)
    prm[:, P_G2:P_G2 + 8] = fm(np.asarray(norm_ffn_g)[0], 8)
    prm[:, P_G3:P_G3 + 8] = fm(np.asarray(final_norm_g), 8)
    prm[:, P_GHG:P_GHG + 8] = fm(np.asarray(hg_norm_g)[0], 8)
    prm[:, P_PSC:P_PSC + 4] = fm(np.asarray(pool_scale)[0], 4)
    prm[:, P_L0:P_L0 + 8] = fm(np.asarray(lb_logits)[0], 8)
    prm[:, P_L1:P_L1 + 8] = fm(np.asarray(lb_logits)[1], 8)
    cf, cfb = make_consts()
    shared = {
        "w_in": np.ascontiguousarray(np.asarray(w_in, np.float32)[0]),
        "w_grp": np.ascontiguousarray(np.asarray(w_pool_grp, np.float32)[0]),
        "w_brh": np.ascontiguousarray(np.asarray(w_br_hgrn, np.float32)[0]),
        "w_brp": np.ascontiguousarray(np.asarray(w_br_pool, np.float32)[0]),
        "w_out": np.ascontiguousarray(np.asarray(w_out, np.float32)[0]),
        "w_fg": np.ascontiguousarray(np.asarray(w_ffn_gate, np.float32)[0]),
        "w_fu": np.ascontiguousarray(np.asarray(w_ffn_up, np.float32)[0]),
        "w_fd": np.ascontiguousarray(np.asarray(w_ffn_down, np.float32)[0]),
        "cf32": cf,
        "cfb32": cfb,
        "prm": prm,
    }
    in_maps = []
    for b in range(_cores):
        hT = np.ascontiguousarray(np.concatenate([meta, x[b]], axis=0).T)
        m = dict(shared)
        m["xT"] = hT
        in_maps.append(m)
    key = _n_tiles
    if key not in _NC_CACHE:
        _NC_CACHE[key] = build_nc(_n_tiles)
    nc = _NC_CACHE[key]
    res = run_bass_kernel_spmd(nc, in_maps, core_ids=list(range(_cores)), trace=_trace)
    out = np.stack([np.ascontiguousarray(r["outT"].T) for r in res.results], axis=0)
    if _trace:
        kernel.last_res = res
    return out.astype(np.float32)
```
